# Optimizing a Trainium2 kernel written in Bass

```python
import math
import jax, jax.numpy as jnp
from jax import lax
import numpy as np

D_MODEL = 1024
BATCH = 32
SEQ = 2048
DEPTH = 1

HEAD_DIM = 64
ATTN_HEADS_PER_GROUP = 8
DILATED_GROUPS = ((128, 1), (512, 4), (2048, 16))
N_GROUPS = 3
ATTN_WIDTH = ATTN_HEADS_PER_GROUP * HEAD_DIM
N_ATTN_HEADS = N_GROUPS * ATTN_HEADS_PER_GROUP
QBLOCK = 128
N_BUCKETS = 32
MAX_DISTANCE = 2048
RWKV_HEADS = 8
RWKV_HEAD_SIZE = 64
RWKV_WIDTH = RWKV_HEADS * RWKV_HEAD_SIZE
DECAY_LORA = 64
AAA_LORA = 64
GN_EPS = 64e-5
RMS_EPS = 1e-6

QKV_COLS = 3 * N_GROUPS * ATTN_WIDTH
OFF_ATTN_GATE = QKV_COLS
OFF_RWKV = OFF_ATTN_GATE + ATTN_WIDTH
RWKV_SHIFT_COLS = 3 * RWKV_WIDTH + DECAY_LORA + AAA_LORA
OFF_RWKV_GATE = OFF_RWKV + RWKV_SHIFT_COLS
OFF_MERGE = OFF_RWKV_GATE + RWKV_WIDTH
IN_COLS = OFF_MERGE + 2 * D_MODEL

kernel_name = "hybrid_dilated_attn_rwkv7_gated_merge"


def rms_norm(x, g):
    xf = x.astype(jnp.float32)
    y = xf * lax.rsqrt(jnp.mean(xf * xf, axis=-1, keepdims=True) + RMS_EPS) * g.astype(jnp.float32)
    return y.astype(x.dtype)


def t5_causal_bucket(dist):
    d = np.maximum(np.asarray(dist), 0)
    max_exact = N_BUCKETS // 2
    ratio = np.log(np.maximum(d, 1).astype(np.float32) / max_exact) / np.float32(math.log(MAX_DISTANCE / max_exact))
    large = max_exact + (ratio * (N_BUCKETS - max_exact)).astype(np.int32)
    large = np.minimum(large, N_BUCKETS - 1)
    return np.where(d < max_exact, d, large).astype(np.int32)


def dilated_group_attention(q, k, v, bias_table_g, window, dilation):
    B, S, H, Dh = q.shape
    L = S // dilation
    steps = window // dilation
    nb = -(-L // QBLOCK)
    Lp = nb * QBLOCK

    def to_blocks(t):
        t = t.reshape(B, L, dilation, H, Dh).transpose(0, 2, 3, 1, 4)
        t = jnp.pad(t, ((0, 0), (0, 0), (0, 0), (0, Lp - L), (0, 0)))
        return t.reshape(B, dilation, H, nb, QBLOCK, Dh)

    def with_prev(t):
        prev = jnp.pad(t, ((0, 0), (0, 0), (0, 0), (1, 0), (0, 0), (0, 0)))[:, :, :, :-1]
        return jnp.concatenate([prev, t], axis=4)

    qb = to_blocks(q)
    kb = with_prev(to_blocks(k))
    vb = with_prev(to_blocks(v))

    qi = np.arange(QBLOCK)[:, None] + QBLOCK
    ki = np.arange(2 * QBLOCK)[None, :]
    rel = qi - ki
    band = (rel >= 0) & (rel <= steps)
    has_prev = (np.arange(nb)[:, None, None] > 0) | (ki >= QBLOCK)[None]
    valid = band[None] & has_prev
    bucket = t5_causal_bucket(np.maximum(rel, 0) * dilation)
    bias = bias_table_g.astype(jnp.float32)[bucket].transpose(2, 0, 1)

    scale = 1.0 / math.sqrt(Dh)
    logits = jnp.einsum('bdhnqc,bdhnkc->bdhnqk', qb, kb) * scale + bias[None, None, :, None]
    logits = jnp.where(valid, logits, -jnp.inf)
    m = jnp.max(logits, axis=-1)
    p = jnp.exp(logits - m[..., None])
    s = jnp.sum(p, axis=-1)
    num = jnp.einsum('bdhnqk,bdhnkc->bdhnqc', p, vb)

    def unblock(t):
        C = t.shape[-1]
        t = t.reshape(B, dilation, H, Lp, C)[:, :, :, :L]
        return t.transpose(0, 3, 1, 2, 4).reshape(B, S, H, C)

    return unblock(num), unblock(s[..., None])[..., 0], unblock(m[..., None])[..., 0]


def rwkv7_scan(r, decay, k, v, a_in, b_in):
    B, S, H, N = r.shape
    xs = tuple(t.transpose(1, 0, 2, 3) for t in (r, decay, k, v, a_in, b_in))

    def step(state, inp):
        r_t, w_t, k_t, v_t, a_t, b_t = inp
        sa = jnp.einsum('bhvk,bhk->bhv', state, a_t)
        state = (state * w_t[:, :, None, :] + sa[..., None] * b_t[:, :, None, :]
                 + v_t[..., None] * k_t[:, :, None, :])
        y = jnp.einsum('bhvk,bhk->bhv', state, r_t)
        return state, y

    s0 = jnp.zeros((B, H, N, N), jnp.float32)
    _, ys = lax.scan(step, s0, xs)
    return ys.transpose(1, 0, 2, 3)


def setup_inputs(seed: int = 0) -> dict:
    key = jax.random.key(seed)
    ks = jax.random.split(key, 20)
    f32 = jnp.float32
    nrm = lambda k, shape: jax.random.normal(k, shape, f32)
    L = DEPTH
    return {
        "x": nrm(ks[0], (BATCH, SEQ, D_MODEL)),
        "pre_norm_gain": 1.0 + 0.05 * nrm(ks[1], (L, D_MODEL)),
        "w_in": nrm(ks[2], (L, D_MODEL, IN_COLS)) * D_MODEL ** -0.5,
        "rel_bias": 0.5 * nrm(ks[3], (N_BUCKETS, N_ATTN_HEADS)),
        "rwkv_shift_mix": jax.random.uniform(ks[4], (L, RWKV_SHIFT_COLS), f32),
        "rwkv_w0": jax.random.uniform(ks[5], (L, RWKV_WIDTH), f32, -6.0, 1.0),
        "rwkv_w_up": nrm(ks[6], (L, DECAY_LORA, RWKV_WIDTH)) * 0.5 * DECAY_LORA ** -0.5,
        "rwkv_a0": 0.5 * nrm(ks[7], (L, RWKV_WIDTH)),
        "rwkv_a_up": nrm(ks[8], (L, AAA_LORA, RWKV_WIDTH)) * 0.5 * AAA_LORA ** -0.5,
        "rwkv_k_k": 0.85 + 0.05 * nrm(ks[9], (L, RWKV_WIDTH)),
        "rwkv_k_a": 1.0 + 0.05 * nrm(ks[10], (L, RWKV_WIDTH)),
        "rwkv_r_k": 0.1 * nrm(ks[11], (L, RWKV_HEADS, RWKV_HEAD_SIZE)),
        "rwkv_ln_w": 1.0 + 0.05 * nrm(ks[12], (L, RWKV_WIDTH)),
        "rwkv_ln_b": 0.01 * nrm(ks[13], (L, RWKV_WIDTH)),
        "w_up_attn": nrm(ks[14], (L, ATTN_WIDTH, D_MODEL)) * ATTN_WIDTH ** -0.5,
        "w_up_rwkv": nrm(ks[15], (L, RWKV_WIDTH, D_MODEL)) * RWKV_WIDTH ** -0.5,
        "w_out": nrm(ks[16], (L, D_MODEL, D_MODEL)) * D_MODEL ** -0.5,
        "post_norm_gain": 1.0 + 0.05 * nrm(ks[17], (L, D_MODEL)),
    }


def reference(x, pre_norm_gain, w_in, rel_bias, rwkv_shift_mix, rwkv_w0, rwkv_w_up,
              rwkv_a0, rwkv_a_up, rwkv_k_k, rwkv_k_a, rwkv_r_k, rwkv_ln_w, rwkv_ln_b,
              w_up_attn, w_up_rwkv, w_out, post_norm_gain):
    B, S, D = x.shape
    f32 = jnp.float32
    for l in range(DEPTH):
        h = rms_norm(x, pre_norm_gain[l])
        proj = h @ w_in[l]

        qkv = proj[..., :QKV_COLS].astype(f32).reshape(
            B, S, 3, N_GROUPS, ATTN_HEADS_PER_GROUP, HEAD_DIM)
        nums, dens, maxes = [], [], []
        for g, (window, dilation) in enumerate(DILATED_GROUPS):
            table_g = rel_bias[:, g * ATTN_HEADS_PER_GROUP:(g + 1) * ATTN_HEADS_PER_GROUP]
            num, den, mx = dilated_group_attention(
                qkv[:, :, 0, g], qkv[:, :, 1, g], qkv[:, :, 2, g], table_g, window, dilation)
            nums.append(num); dens.append(den); maxes.append(mx)
        m_all = jnp.max(jnp.stack(maxes), axis=0)
        wts = [jnp.exp(mx - m_all) for mx in maxes]
        num_tot = sum(w[..., None] * n for w, n in zip(wts, nums))
        den_tot = sum(w * d for w, d in zip(wts, dens))
        o_attn = (num_tot / den_tot[..., None]).reshape(B, S, ATTN_WIDTH).astype(x.dtype)
        z_attn = proj[..., OFF_ATTN_GATE:OFF_RWKV]
        y_attn = (o_attn * jax.nn.silu(z_attn)) @ w_up_attn[l]

        pr = proj[..., OFF_RWKV:OFF_RWKV_GATE].astype(f32)
        pr_prev = jnp.pad(pr, ((0, 0), (1, 0), (0, 0)))[:, :-1]
        pr = pr + (pr_prev - pr) * rwkv_shift_mix[l].astype(f32)
        r = pr[..., :RWKV_WIDTH]
        k = pr[..., RWKV_WIDTH:2 * RWKV_WIDTH]
        v = pr[..., 2 * RWKV_WIDTH:3 * RWKV_WIDTH]
        w_low = pr[..., 3 * RWKV_WIDTH:3 * RWKV_WIDTH + DECAY_LORA]
        a_low = pr[..., 3 * RWKV_WIDTH + DECAY_LORA:]
        w_log = -jax.nn.softplus(-(rwkv_w0[l].astype(f32) + jnp.tanh(w_low) @ rwkv_w_up[l].astype(f32))) - 0.5
        decay = jnp.exp(-jnp.exp(w_log))
        a = jax.nn.sigmoid(rwkv_a0[l].astype(f32) + a_low @ rwkv_a_up[l].astype(f32))
        hs = (B, S, RWKV_HEADS, RWKV_HEAD_SIZE)
        kk = (k * rwkv_k_k[l].astype(f32)).reshape(hs)
        kk = kk / jnp.maximum(jnp.linalg.norm(kk, axis=-1, keepdims=True), 1e-12)
        k = k * (1.0 + (a - 1.0) * rwkv_k_a[l].astype(f32))
        r4, k4, v4 = r.reshape(hs), k.reshape(hs), v.reshape(hs)
        y = rwkv7_scan(r4, decay.reshape(hs), k4, v4, -kk, kk * a.reshape(hs))
        mu = jnp.mean(y, axis=-1, keepdims=True)
        var = jnp.mean(jnp.square(y - mu), axis=-1, keepdims=True)
        y = ((y - mu) * lax.rsqrt(var + GN_EPS)).reshape(B, S, RWKV_WIDTH)
        y = y * rwkv_ln_w[l].astype(f32) + rwkv_ln_b[l].astype(f32)
        bonus = jnp.sum(r4 * k4 * rwkv_r_k[l].astype(f32), axis=-1, keepdims=True) * v4
        o_rwkv = (y + bonus.reshape(B, S, RWKV_WIDTH)).astype(x.dtype)
        z_rwkv = proj[..., OFF_RWKV_GATE:OFF_MERGE]
        y_rwkv = (o_rwkv * jax.nn.silu(z_rwkv)) @ w_up_rwkv[l]

        g_attn = proj[..., OFF_MERGE:OFF_MERGE + D_MODEL]
        g_rwkv = proj[..., OFF_MERGE + D_MODEL:]
        merged = jax.nn.sigmoid(g_attn) * y_attn + jax.nn.sigmoid(g_rwkv) * y_rwkv
        out = merged @ w_out[l]
        x = x + rms_norm(out, post_norm_gain[l])
    return x
```

```python
from contextlib import ExitStack
import math
import numpy as np
import concourse.bass as bass
import concourse.mybir as mybir
from concourse.bass_utils import run_bass_kernel_spmd

F32 = mybir.dt.float32
BF16 = mybir.dt.bfloat16
ALU = mybir.AluOpType
AF = mybir.ActivationFunctionType
ENGS = ("pe", "act", "dve", "pool", "sp")

NCORES = 8
NSEQ = 4
S = 2048
D = 1024
NTILE = 73
C0 = math.exp(-0.5)
RMS_EPS = 1e-6
GN_EPS = 64e-5
NEG = -30000.0


class Buf:
    def __init__(self, t, name):
        self.t = t
        self.name = name
        self.last_write = None
        self.readers = {}
        self.dsem = None

    def __getitem__(self, idx):
        return self.t[idx]


class Prog:
    def __init__(self, nc):
        self.nc = nc
        self.stack = ExitStack()
        self.q = {e: [] for e in ENGS}
        self.cnt = {}
        self.waited = {e: {} for e in ENGS}
        self.sems = {}
        self.nbuf = 0
        self.dsems = {}
        self.phase = "setup"
        self.labels = {e: [] for e in ENGS}
        for e in ENGS:
            self._sem("E_" + e)

    def _sem(self, key):
        if key not in self.sems:
            self.sems[key] = self.stack.enter_context(self.nc.semaphore("s%d" % len(self.sems)))
            self.cnt[key] = 0
        return key

    def sbuf(self, name, shape, dtype):
        t = self.stack.enter_context(self.nc.sbuf_tensor(name, list(shape), dtype))
        return Buf(t, name)

    def psum(self, name, shape, dtype):
        t = self.stack.enter_context(self.nc.psum_tensor(name, list(shape), dtype))
        b = Buf(t, name)
        b.is_psum = True
        return b

    def dram(self, name, shape, dtype, kind):
        t = self.nc.dram_tensor(name, list(shape), dtype, kind=kind).ap()
        return Buf(t, name)

    def view(self, ap, name):
        return Buf(ap, name)

    def _deps(self, eng, reads, writes, is_dma):
        own = "E_" + eng
        deps = {}

        def add(k, v):
            if deps.get(k, 0) < v:
                deps[k] = v

        for b in reads:
            if b.last_write is not None:
                k, v = b.last_write
                if not (k == own and eng == "pe"):
                    add(k, v)
            if getattr(b, "is_psum", False):
                for k, v in b.readers.items():
                    if k != own:
                        add(k, v)
        for b in writes:
            if b.last_write is not None:
                k, v = b.last_write
                if not (k == own and eng == "pe"):
                    add(k, v)
            for k, v in b.readers.items():
                if k == own and eng == "pe":
                    continue
                add(k, v)
        w = self.waited[eng]
        out = []
        for k, v in deps.items():
            if w.get(k, 0) < v:
                w[k] = v
                out.append((k, v))
        return out

    def _commit(self, tok, reads, writes):
        k, v = tok
        for b in reads:
            if b.readers.get(k, 0) < v:
                b.readers[k] = v
        for b in writes:
            b.last_write = tok
            b.readers = {}

    def op(self, eng, fn, reads=(), writes=()):
        reads = [b for b in reads if b is not None]
        writes = [b for b in writes if b is not None]
        waits = self._deps(eng, reads, writes, False)
        key = "E_" + eng
        self.cnt[key] += 1
        tok = (key, self.cnt[key])
        self.q[eng].append((waits, fn, key, 1))
        self.labels[eng].append(self.phase)
        self._commit(tok, reads, writes)
        return tok

    def dma(self, eng, out_ap, in_ap, reads=(), writes=(), sembuf=None):
        reads = [b for b in reads if b is not None]
        writes = [b for b in writes if b is not None]
        waits = self._deps(eng, reads, writes, True)
        sb = sembuf
        nk = (sb.name, eng)
        if nk not in self.dsems:
            self.dsems[nk] = self._sem("D_%d" % self.nbuf)
            self.nbuf += 1
        key = self.dsems[nk]
        self.cnt[key] += 16
        tok = (key, self.cnt[key])

        def fn(e, out_ap=out_ap, in_ap=in_ap):
            return e.dma_start(out=out_ap, in_=in_ap)

        self.q[eng].append((waits, fn, key, 16))
        self._commit(tok, reads, writes)
        return tok

    def alias(self, olds, news):
        merged = {}
        for b in olds:
            if b.last_write is not None:
                k, v = b.last_write
                merged[k] = max(merged.get(k, 0), v)
            for k, v in b.readers.items():
                merged[k] = max(merged.get(k, 0), v)
        for b in news:
            for k, v in merged.items():
                if b.readers.get(k, 0) < v:
                    b.readers[k] = v

    def wait_all(self, eng, toks):
        w = self.waited[eng]
        waits = []
        for k, v in toks:
            if w.get(k, 0) < v:
                w[k] = v
                waits.append((k, v))
        self.q[eng].append((waits, None, None, 0))

    def emit(self):
        nc = self.nc
        sems = self.sems
        qs = self.q
        with nc.Block() as block:
            def run(eng_name):
                def body(e):
                    for waits, fn, key, inc in qs[eng_name]:
                        for k, v in waits:
                            e.wait_ge(sems[k], v)
                        if fn is not None:
                            fn(e).then_inc(sems[key], inc)
                return body

            block.tensor(run("pe"))
            block.scalar(run("act"))
            block.vector(run("dve"))
            block.gpsimd(run("pool"))
            block.sync(run("sp"))

    def close(self):
        self.stack.close()


def _t5_bucket(dist):
    d = np.maximum(np.asarray(dist), 0)
    max_exact = 16
    ratio = np.log(np.maximum(d, 1).astype(np.float32) / max_exact) / np.float32(math.log(2048 / max_exact))
    large = max_exact + (ratio * (32 - max_exact)).astype(np.int32)
    large = np.minimum(large, 31)
    return np.where(d < max_exact, d, large).astype(np.int32)


def _bias_tables(rel_bias):
    lk = np.arange(128)[:, None]
    lq = np.arange(256)[None, :]
    rel = lq - lk
    valid = (rel >= 0) & (rel <= 128)
    out = np.empty((24, 128, 256), np.float32)
    for g, dil in enumerate((1, 4, 16)):
        bucket = _t5_bucket(np.maximum(rel, 0) * dil)
        for h in range(8):
            tab = rel_bias[:, g * 8 + h][bucket]
            out[g * 8 + h] = np.where(valid, tab, np.float32(NEG))
    return out


def _const_tables():
    c = {}
    c["ident"] = np.eye(128, dtype=np.float32)
    blk = np.zeros((128, 128), np.float32)
    blk[:64, :64] = 1.0
    blk[64:, 64:] = 1.0
    c["blk1"] = blk
    j = np.arange(128)[:, None]
    t = np.arange(128)[None, :]
    same = (j // 64) == (t // 64)
    su = (same & (j < t)).astype(np.float32)
    iu = (same & (j <= t)).astype(np.float32)
    sl = (same & (j > t)).astype(np.float32)
    c["mu2"] = np.concatenate([su, iu], axis=1)
    c["ml2"] = np.concatenate([sl, sl], axis=1)
    return c


def build(nseq=NSEQ):
    nc = bass.Bass("TRN2", target_bir_lowering=False)
    P = Prog(nc)
    x_d = P.dram("x", [nseq, S, D], F32, "ExternalInput")
    win_d = P.dram("w_in", [D, NTILE * 128], F32, "ExternalInput")
    bias_d = P.dram("biasm", [24, 128, 256], F32, "ExternalInput")
    pcol_d = P.dram("pcol", [128, 80], F32, "ExternalInput")
    pg_d = P.dram("pgain", [1, D], F32, "ExternalInput")
    lora_d = P.dram("lora", [128, 512], F32, "ExternalInput")
    wua_d = P.dram("wua", [512, D], F32, "ExternalInput")
    wur_d = P.dram("wur", [512, D], F32, "ExternalInput")
    wout_d = P.dram("wout", [D, D], F32, "ExternalInput")
    cst_d = P.dram("cst", [128, 768], F32, "ExternalInput")
    out_d = P.dram("out", [nseq, S, D], F32, "ExternalOutput")
    wbf_d = [P.dram("wbf%d" % i, [128, 1024], BF16, "Internal") for i in range(NTILE)]

    hT = P.sbuf("hT", [128, 8 * S], BF16)
    hT3 = hT[:, :].rearrange("p (k t) -> p k t", t=S)
    NSLOT = 3
    wsl = [P.sbuf("wsl%d" % i, [128, 1024], BF16) for i in range(NSLOT)]
    wua = P.sbuf("wua_s", [128, 4 * D], BF16)
    wur = P.sbuf("wur_s", [128, 4 * D], BF16)
    wout = P.sbuf("wout_s", [128, 8 * D], BF16)
    lora = P.sbuf("lora_s", [128, 512], BF16)
    pcol = P.sbuf("pcol_s", [128, 80], F32)
    cstf = P.sbuf("cstf", [128, 768], F32)
    identb = P.sbuf("identb", [128, 128], BF16)
    blk1 = P.sbuf("blk1b", [128, 128], BF16)
    blkavg = P.sbuf("blkavg", [128, 128], BF16)
    onesf = P.sbuf("onesf", [128, 64], F32)
    ones5 = P.sbuf("ones5", [128, 512], BF16)
    oaT = [P.sbuf("oaT%d" % i, [128, S], BF16) for i in range(4)]
    orT = [P.sbuf("orT%d" % i, [128, S], BF16) for i in range(4)]
    Sf = [P.sbuf("Sf%d" % i, [128, 128], F32) for i in range(4)]
    Sb = [P.sbuf("Sb%d" % i, [128, 128], BF16) for i in range(4)]
    prevc = P.sbuf("prevc", [128, 16], F32)
    small = P.sbuf("small", [128, 16], F32)
    U = P.sbuf("U", [128, 23552], F32)

    pb = [P.psum("pb%d" % i, [128, 512], F32) for i in range(8)]

    PC_GAIN, PC_MIX, PC_W0, PC_A0, PC_KK, PC_KA, PC_RK, PC_LNW, PC_LNB, PC_OMKA = 0, 8, 21, 25, 29, 33, 37, 41, 45, 49
    PC_OMIX = 56

    def carve(specs):
        off = 0
        res = {}
        for name, ncols, dt in specs:
            words = ncols if dt == F32 else (ncols + 1) // 2
            ap = U[:, off:off + words]
            if dt == BF16:
                ap = ap.bitcast(BF16)
            res[name] = Buf(ap, name)
            off += words
        assert off <= 23552, off
        return res

    state = {"phase": []}

    def new_phase(specs):
        bufs = carve(specs)
        P.alias(state["phase"] + [U], list(bufs.values()))
        state["phase"] = list(bufs.values())
        return bufs

    def mm(out_ap, lhsT, rhs, reads, writes, start=True, stop=True, skip=False):
        P.op("pe", lambda e: e.matmul(out_ap, lhsT=lhsT, rhs=rhs, start=start, stop=stop, skip_group_check=skip), reads, writes)

    def act(out_ap, in_ap, func, reads, writes, **kw):
        P.op("act", lambda e: e.activation(out=out_ap, in_=in_ap, func=func, **kw), reads, writes)

    def acopy(out_ap, in_ap, reads, writes):
        P.op("act", lambda e: e.copy(out=out_ap, in_=in_ap), reads, writes)

    def tt(eng, out_ap, in0, in1, op, reads, writes):
        P.op(eng, lambda e: e.tensor_tensor(out=out_ap, in0=in0, in1=in1, op=op), reads, writes)

    def ts(eng, out_ap, in0, s1, s2, op0, op1, reads, writes):
        if op1 is None:
            P.op(eng, lambda e: e.tensor_scalar(out=out_ap, in0=in0, scalar1=s1, scalar2=None, op0=op0), reads, writes)
        else:
            P.op(eng, lambda e: e.tensor_scalar(out=out_ap, in0=in0, scalar1=s1, scalar2=s2, op0=op0, op1=op1), reads, writes)

    def stt(out_ap, in0, scalar, in1, op0, op1, reads, writes):
        P.op("dve", lambda e: e.scalar_tensor_tensor(out=out_ap, in0=in0, scalar=scalar, in1=in1, op0=op0, op1=op1), reads, writes)

    def vcopy(eng, out_ap, in_ap, reads, writes):
        P.op(eng, lambda e: e.tensor_copy(out=out_ap, in_=in_ap), reads, writes)

    def recip(out_ap, in_ap, reads, writes):
        P.op("dve", lambda e: e.reciprocal(out=out_ap, in_=in_ap), reads, writes)

    wctr = [0]

    def get_w(tile):
        slot = wsl[wctr[0] % NSLOT]
        wctr[0] += 1
        P.dma("sp", slot[:, :], wbf_d[tile][:, :], reads=[wbf_d[tile]], writes=[slot], sembuf=slot)
        return slot

    def proj_fm(slot, tok0, n, bank, m0=0, m1=128):
        w3 = slot[:, :].rearrange("p (k c) -> p k c", c=128)
        for kc in range(8):
            mm(bank[0:m1 - m0, 0:n], w3[:, kc, m0:m1], hT3[:, kc, tok0:tok0 + n], [slot, hT], [bank],
               start=(kc == 0), stop=(kc == 7))

    P.dma("pool", pcol[:, :], pcol_d[:, :], reads=[pcol_d], writes=[pcol], sembuf=pcol)
    P.dma("pool", cstf[:, :], cst_d[:, :], reads=[cst_d], writes=[cstf], sembuf=cstf)
    vcopy("dve", identb[:, :], cstf[:, 0:128], [cstf], [identb])
    vcopy("dve", blk1[:, :], cstf[:, 128:256], [cstf], [blk1])
    ts("dve", blkavg[:, :], cstf[:, 128:256], 1.0 / 64.0, None, ALU.mult, None, [cstf], [blkavg])
    P.op("dve", lambda e: e.memset(onesf[:, :], 1.0), [], [onesf])
    P.op("dve", lambda e: e.memset(ones5[:, :], 1.0), [], [ones5])
    ts("dve", pcol[:, PC_OMKA:PC_OMKA + 4], pcol[:, PC_KA:PC_KA + 4], -1.0, 1.0, ALU.mult, ALU.add, [pcol], [pcol])
    ts("dve", pcol[:, PC_OMIX:PC_OMIX + 13], pcol[:, PC_MIX:PC_MIX + 13], -1.0, 1.0, ALU.mult, ALU.add, [pcol], [pcol])

    ph = new_phase([("stg0", 1024, F32), ("stg1", 1024, F32), ("cb0", 1024, BF16), ("cb1", 1024, BF16)])
    stg = [ph["stg0"], ph["stg1"]]
    cb = [ph["cb0"], ph["cb1"]]
    def load_resident(dst, src_d, nchunk, ncols):
        for kc in range(nchunk):
            for hf in range(ncols // 1024):
                s_ = stg[(kc + hf) % 2]
                P.dma("sp", s_[:, :], src_d[kc * 128:(kc + 1) * 128, hf * 1024:(hf + 1) * 1024], reads=[src_d], writes=[s_], sembuf=s_)
                vcopy("dve", dst[:, kc * ncols + hf * 1024: kc * ncols + (hf + 1) * 1024], s_[:, :], [s_], [dst])

    load_resident(wua, wua_d, 4, D)
    load_resident(wur, wur_d, 4, D)
    load_resident(wout, wout_d, 8, D)
    P.dma("sp", stg[0][:, 0:512], lora_d[:, :], reads=[lora_d], writes=[stg[0]], sembuf=stg[0])
    vcopy("dve", lora[:, :], stg[0][:, 0:512], [stg[0]], [lora])
    wua3 = wua[:, :].rearrange("p (k c) -> p k c", c=D)
    wur3 = wur[:, :].rearrange("p (k c) -> p k c", c=D)
    wout3 = wout[:, :].rearrange("p (k c) -> p k c", c=D)

    out_toks = []

    for s in range(nseq):
        P.phase = "X%d" % s
        ph = new_phase([("xb0", 1024, F32), ("xb1", 1024, F32), ("xs", 1024, BF16), ("junk", 1024, BF16)])
        xb = [ph["xb0"], ph["xb1"]]
        xs, junk = ph["xs"], ph["junk"]
        for tt_ in range(16):
            xt = xb[tt_ % 2]
            P.dma("pool", xt[:, :], x_d[s, tt_ * 128:(tt_ + 1) * 128, :], reads=[x_d], writes=[xt], sembuf=xt)
            act(junk[:, :], xt[:, :], AF.Square, [xt], [junk, small], accum_out=small[:, 0:1])
            act(small[:, 1:2], small[:, 0:1], AF.Sqrt, [small], [small], scale=1.0 / D, bias=RMS_EPS)
            recip(small[:, 2:3], small[:, 1:2], [small], [small])
            ts("dve", xs[:, :], xt[:, :], small[:, 2:3], None, ALU.mult, None, [xt, small], [xs])
            bank = pb[6 + tt_ % 2]
            pbf = bank[:, :].bitcast(BF16)
            for kc in range(8):
                P.op("pe", lambda e, kc=kc, pbf=pbf: e.transpose(out=pbf[:, kc * 128:(kc + 1) * 128], in_=xs[:, kc * 128:(kc + 1) * 128], identity=identb[:, :]),
                     [xs, identb], [bank])
            tt("dve", hT3[:, :, tt_ * 128:(tt_ + 1) * 128], pbf.rearrange("p (k t) -> p k t", t=128),
               pcol[:, PC_GAIN:PC_GAIN + 8].unsqueeze(2).broadcast_to([128, 8, 128]), ALU.mult, [bank, pcol], [hT])

        if s == 0:
            order = []
            for hp_ in range(4):
                for g_ in range(3):
                    order += [(w_ * 3 + g_) * 4 + hp_ for w_ in range(3)]
                order.append(36 + hp_)
            order += [t_ for t_ in range(NTILE) if t_ not in order]
            for i_, t in enumerate(order):
                src = win_d[:, t * 128:(t + 1) * 128].rearrange("(k p) j -> p k j", p=128)
                lead = order[i_ - i_ % 2]
                P.dma("pool", wbf_d[t][:, :].rearrange("p (k j) -> p k j", j=128), src, reads=[win_d], writes=[wbf_d[t]], sembuf=wbf_d[lead])
                if i_ % 2 == 1:
                    wbf_d[lead].last_write = wbf_d[t].last_write

        P.phase = "A%d" % s
        specs = []
        for g in range(3):
            specs += [("qT%d" % g, S, BF16), ("kT%d" % g, S, BF16), ("vb%d" % g, 16 * 192, BF16)]
        specs += [("gz", S, BF16), ("bias", 6 * 256, F32), ("tmp0", 256, F32), ("tmp1", 256, F32), ("tmp2", 256, F32), ("tmp3", 256, F32),
                  ("pT0", 256, BF16), ("pT1", 256, BF16), ("pT2", 256, BF16), ("pT3", 256, BF16), ("rden", 512, F32), ("bcs", 512, F32), ("t1", 512, F32)]
        ph = new_phase(specs)
        qT = [ph["qT%d" % g] for g in range(3)]
        kT = [ph["kT%d" % g] for g in range(3)]
        vb = [ph["vb%d" % g] for g in range(3)]
        gz, biasb, rden, bcs, t1 = ph["gz"], ph["bias"], ph["rden"], ph["bcs"], ph["t1"]
        tmpb = [ph["tmp%d" % i] for i in range(4)]
        pTb = [ph["pT%d" % i] for i in range(4)]
        for g in range(3):
            v3 = vb[g][:, :].rearrange("p (b c) -> p b c", c=192)
            P.op("pool", lambda e, v3=v3: e.memset(v3[:, :, 64:65], 1.0), [], [vb[g]])
            P.op("pool", lambda e, v3=v3: e.memset(v3[:, :, 65:128], 0.0), [], [vb[g]])
        pctr = [0]

        def pbank():
            pctr[0] += 1
            return pb[6 + pctr[0] % 2]

        for hp in range(4):
            for g in range(3):
                for hh in range(2):
                    i6 = g * 2 + hh
                    P.dma("pool", biasb[:, i6 * 256:(i6 + 1) * 256], bias_d[g * 8 + hp * 2 + hh, :, :], reads=[bias_d], writes=[biasb], sembuf=biasb)
            for g, dil in enumerate((1, 4, 16)):
                L = S // dil
                for which, dst in ((0, qT[g]), (1, kT[g])):
                    slot = get_w((which * 3 + g) * 4 + hp)
                    for b4 in range(4):
                        bank = pbank()
                        proj_fm(slot, b4 * 512, 512, bank)
                        if dil == 1:
                            acopy(dst[:, b4 * 512:(b4 + 1) * 512], bank[:, :], [bank], [dst])
                        else:
                            n = 512 // dil
                            o3 = dst[:, :].rearrange("p (r l) -> p r l", r=dil)[:, :, b4 * n:(b4 + 1) * n]
                            i3 = bank[:, :].rearrange("p (l r) -> p r l", r=dil)
                            acopy(o3, i3, [bank], [dst])
                slot = get_w((6 + g) * 4 + hp)
                w3 = slot[:, :].rearrange("p (k c) -> p k c", c=128)
                v3 = vb[g][:, :].rearrange("p (b c) -> p b c", c=192)
                nbj = 16 // dil
                for q4 in range(4):
                    bank = pbank()
                    for i4 in range(4):
                        kb = q4 * 4 + i4
                        rho, j = kb // nbj, kb % nbj
                        t0 = rho + dil * 128 * j
                        for kc in range(8):
                            mm(bank[:, i4 * 128:(i4 + 1) * 128], hT3[:, kc, t0:t0 + dil * 127 + 1:dil], w3[:, kc, :], [hT, slot], [bank],
                               start=(kc == 0), stop=(kc == 7))
                    b3 = bank[:, :].rearrange("p (b c) -> p b c", c=128)
                    acopy(v3[:, q4 * 4:(q4 + 1) * 4, 0:64], b3[:, :, 0:64], [bank], [vb[g]])
                    vcopy("dve", v3[:, q4 * 4:(q4 + 1) * 4, 128:192], b3[:, :, 64:128], [bank], [vb[g]])
            slot = get_w(36 + hp)
            for b4 in range(4):
                bank = pbank()
                proj_fm(slot, b4 * 512, 512, bank)
                act(gz[:, b4 * 512:(b4 + 1) * 512], bank[:, :], AF.Silu, [bank], [gz])

            for hh in range(2):
                r0 = 64 * hh
                hrows = slice(r0, r0 + 64)
                if hh == 0:
                    orows, lcols, drow = slice(0, 65), slice(0, 65), 64
                else:
                    orows, lcols, drow = slice(0, 128), slice(64, 192), 0
                nrows = slice(r0, r0 + 64)
                started = [False] * 4
                plan = []
                for g, dil in ((2, 16), (1, 4), (0, 1)):
                    L = S // dil
                    nbj = 16 // dil
                    for kb in range(16):
                        rho, j = kb // nbj, kb % nbj
                        nq = 256 if j + 1 < nbj else 128
                        avs = []
                        for m_ in range(nq // 128):
                            jq = j + m_
                            if dil == 1:
                                avs.append((jq // 4, slice((jq % 4) * 128, (jq % 4) * 128 + 128), slice(m_ * 128, m_ * 128 + 128)))
                            elif dil == 4:
                                avs.append((jq, slice(rho, rho + 4 * 127 + 1, 4), slice(m_ * 128, m_ * 128 + 128)))
                            else:
                                for b_ in range(4):
                                    avs.append((b_, slice(rho, rho + 16 * 31 + 1, 16), slice(b_ * 32, b_ * 32 + 32)))
                        plan.append((g, kb, rho, j, nq, avs))
                last = {}
                for pi, (g, kb, rho, j, nq, avs) in enumerate(plan):
                    for ai, a_ in enumerate(avs):
                        last[a_[0]] = (pi, ai)
                SKEW = 3

                def emit_scores(pi):
                    g, kb, rho, j, nq, avs = plan[pi]
                    dil = (1, 4, 16)[g]
                    L = S // dil
                    sb_ = pb[4 + pi % 4]
                    tm = tmpb[pi % 4]
                    pT = pTb[pi % 4]
                    kcol = rho * L + j * 128
                    mm(sb_[:, 0:nq], kT[g][hrows, kcol:kcol + 128], qT[g][hrows, kcol:kcol + nq], [kT[g], qT[g]], [sb_])
                    i6 = g * 2 + hh
                    stt(tm[:, 0:nq], sb_[:, 0:nq], 0.125, biasb[:, i6 * 256:i6 * 256 + nq], ALU.mult, ALU.add, [sb_, biasb], [tm])
                    act(pT[:, 0:nq], tm[:, 0:nq], AF.Exp, [tm], [pT])

                def emit_av(pi):
                    g, kb, rho, j, nq, avs = plan[pi]
                    pT = pTb[pi % 4]
                    v3 = vb[g][:, :].rearrange("p (b c) -> p b c", c=192)
                    for ai, (bk, ocols, pcols) in enumerate(avs):
                        st = not started[bk]
                        started[bk] = True
                        sp_ = last[bk] == (pi, ai)
                        mm(pb[bk][orows, ocols], v3[:, kb, lcols], pT[:, pcols], [vb[g], pT], [pb[bk]], start=st, stop=sp_, skip=True)

                for pi in range(len(plan) + SKEW):
                    if pi < len(plan):
                        emit_scores(pi)
                    if pi - SKEW >= 0:
                        emit_av(pi - SKEW)
                for bk in range(4):
                    act(rden[drow:drow + 1, :], pb[bk][drow:drow + 1, :], AF.Ln, [pb[bk]], [rden])
                    act(rden[drow:drow + 1, :], rden[drow:drow + 1, :], AF.Exp, [rden], [rden], scale=-1.0)
                    bb = pb[4 + bk % 2]
                    mm(bb[nrows, :], onesf[drow:drow + 1, 0:64], rden[drow:drow + 1, :], [onesf, rden], [bb])
                    acopy(bcs[nrows, :], bb[nrows, :], [bb], [bcs])
                    tt("dve", t1[nrows, :], pb[bk][nrows, :], bcs[nrows, :], ALU.mult, [pb[bk], bcs], [t1])
                    tt("pool", oaT[hp][nrows, bk * 512:(bk + 1) * 512], t1[nrows, :], gz[nrows, bk * 512:(bk + 1) * 512], ALU.mult, [t1, gz], [oaT[hp]])

        P.phase = "R%d" % s
        specs = [("rA", 514, F32), ("rB", 512, F32), ("rr", 512, F32), ("kk0", 512, F32), ("vv", 512, F32),
                 ("rF", 512, F32), ("aa", 512, F32), ("rH", 514, F32), ("Pinv", 512, F32), ("Pex", 512, F32),
                 ("Pend", 512, F32), ("rL", 512, F32), ("beta", 512, F32), ("rN", 512, F32),
                 ("bon0", 512, F32), ("bon1", 512, F32), ("lowm", 512, F32), ("lowbf", 512, BF16), ("PCc0", 8, F32), ("PCc1", 8, F32),
                 ("sq", 512, BF16),
                 ("ALRT", 2048, BF16), ("BTKT", 2048, BF16), ("BH", 1024, BF16), ("KH", 1024, BF16), ("VB", 1024, BF16),
                 ("Yb", 512, F32), ("yc", 512, F32), ("zr", 512, F32), ("Ybf", 512, BF16), ("Ut", 128, BF16)]
        for g_ in range(4):
            specs += [("AbAr%d" % g_, 512, BF16), ("TRg%d" % g_, 1024, BF16), ("QWg%d" % g_, 512, BF16),
                      ("ATg%d" % g_, 512, BF16), ("TXa%d" % g_, 512, BF16), ("TXb%d" % g_, 512, BF16)]
        for g_ in range(2):
            specs += [("Akr%d" % g_, 512, BF16), ("XTa%d" % g_, 512, BF16), ("XTb%d" % g_, 512, BF16)]
        ph = new_phase(specs)
        R = ph
        for g_ in range(4):
            for ab in "ab":
                t_ = R["TX%s%d" % (ab, g_)]
                x_ = Buf(t_.t, "TX%sx%d" % (ab, g_))
                x_.readers = dict(t_.readers)
                R["TX%sx%d" % (ab, g_)] = x_
        mu2 = Buf(cstf[:, 256:512], "mu2")
        ml2 = Buf(cstf[:, 512:768], "ml2")
        mu2.last_write = cstf.last_write
        ml2.last_write = cstf.last_write
        for nm in ("ALRT", "BTKT", "BH", "KH", "VB"):
            P.op("pool", lambda e, b=R[nm]: e.memset(b[:, :], 0.0), [], [R[nm]])
        P.op("dve", lambda e: e.memset(prevc[:, :], 0.0), [], [prevc])
        for hp in range(4):
            P.op("pool", lambda e, b=Sf[hp]: e.memset(b[:, :], 0.0), [], [Sf[hp]])
            P.op("pool", lambda e, b=Sb[hp]: e.memset(b[:, :], 0.0), [], [Sb[hp]])
        hb = pb[0:6]
        rctr = [0]

        def rbank():
            rctr[0] += 1
            return hb[rctr[0] % 4]

        gpc = [0]

        def gp_bank():
            gpc[0] += 1
            return pb[6 + gpc[0] % 2]

        raw, dtmp = R["rA"], R["rB"]

        def shifted_proj_g(tile, pidx, tok0, dst, mixcol):
            slot = get_w(tile)
            bank = gp_bank()
            proj_fm(slot, tok0, 512, bank)
            yield
            vcopy("dve", raw[:, 0:1], prevc[:, pidx:pidx + 1], [prevc], [raw])
            acopy(raw[:, 1:513], bank[:, :], [bank], [raw])
            yield
            vcopy("dve", prevc[:, pidx:pidx + 1], raw[:, 512:513], [raw], [prevc])
            omix = PC_OMIX + (mixcol - PC_MIX)
            act(dtmp[:, :], raw[:, 1:513], AF.Identity, [raw, pcol], [dtmp], scale=pcol[:, omix:omix + 1], bias=0.0)
            yield
            stt(dst[:, :], raw[:, 0:512], pcol[:, mixcol:mixcol + 1], dtmp[:, :], ALU.mult, ALU.add, [dtmp, pcol, raw], [dst])
            yield

        def c3(buf, w=64):
            return buf[:, :].rearrange("p (c t) -> p c t", t=w)

        ALRT3 = R["ALRT"][:, :].rearrange("p (c w) -> p c w", w=256)
        BTKT3 = R["BTKT"][:, :].rearrange("p (c w) -> p c w", w=256)
        BH3 = R["BH"][:, :].rearrange("p (c w) -> p c w", w=128)
        KH3 = R["KH"][:, :].rearrange("p (c w) -> p c w", w=128)
        VB3 = R["VB"][:, :].rearrange("p (c w) -> p c w", w=128)

        def gP(hp, tok0, first, fl):
            rr, kk0, vv = R["rr"], R["kk0"], R["vv"]
            if first:
                yield from shifted_proj_g(52, 12, tok0, R["lowm"], PC_MIX + 12)
                act(R["lowbf"][0:64, :], R["lowm"][0:64, :], AF.Tanh, [R["lowm"]], [R["lowbf"]])
                vcopy("dve", R["lowbf"][64:128, :], R["lowm"][64:128, :], [R["lowm"]], [R["lowbf"]])
            fl["low"] = True
            yield
            yield from shifted_proj_g(44 + hp, 4 + hp, tok0, kk0, PC_MIX + 4 + hp)
            fl["kk0"] = True
            yield from shifted_proj_g(40 + hp, hp, tok0, rr, PC_MIX + hp)
            yield from shifted_proj_g(48 + hp, 8 + hp, tok0, vv, PC_MIX + 8 + hp)
            fl["P"] = True

        def gD(hp, par, fl):
            while not fl.get("low"):
                yield
            sg, aa, csg = R["rF"], R["aa"], R["rH"]
            bw = pb[4]
            mm(bw[:, :], lora[0:64, hp * 128:(hp + 1) * 128], R["lowbf"][0:64, :], [lora, R["lowbf"]], [bw])
            yield
            act(sg[:, :], bw[:, :], AF.Sigmoid, [bw, pcol], [sg], bias=pcol[:, PC_W0 + hp:PC_W0 + hp + 1], scale=1.0)
            yield
            ba = pb[4]
            mm(ba[:, :], lora[64:128, hp * 128:(hp + 1) * 128], R["lowbf"][64:128, :], [lora, R["lowbf"]], [ba])
            yield
            act(aa[:, :], ba[:, :], AF.Sigmoid, [ba, pcol], [aa], bias=pcol[:, PC_A0 + hp:PC_A0 + hp + 1], scale=1.0)
            fl["aa"] = True
            yield
            P.op("dve", lambda e: e.memset(csg[:, 0:1], 0.0), [], [csg])
            P.op("dve", lambda e: e.tensor_tensor_scan(out=csg[:, 1:513], data0=ones5[:, :], data1=sg[:, :], initial=0.0,
                                                       op0=ALU.mult, op1=ALU.add), [ones5, sg], [csg])
            yield
            while not fl.get("P"):
                yield
            ci, ce, dts = R["rA"], R["rB"], R["rF"]
            base = csg[:, 0:512].rearrange("p (c t) -> p c t", t=64)[:, :, 0:1].broadcast_to([128, 8, 64])
            tt("dve", ci[:, 0:512].rearrange("p (c t) -> p c t", t=64), csg[:, 1:513].rearrange("p (c t) -> p c t", t=64), base, ALU.subtract, [csg], [ci])
            yield
            tt("dve", c3(ce), csg[:, 0:512].rearrange("p (c t) -> p c t", t=64), base, ALU.subtract, [csg], [ce])
            yield
            ci3 = ci[:, 0:512].rearrange("p (c t) -> p c t", t=64)
            tt("dve", c3(dts), ci3[:, :, 63:64].broadcast_to([128, 8, 64]), ci3, ALU.subtract, [ci], [dts])
            yield
            Pm, Pinv, Pex, Pend = R["rH"], R["Pinv"], R["Pex"], R["Pend"]
            PCc = R["PCc%d" % par]
            act(Pinv[:, :], ci[:, 0:512], AF.Exp, [ci], [Pinv], scale=C0)
            yield
            act(Pex[:, :], ce[:, :], AF.Exp, [ce], [Pex], scale=-C0)
            yield
            act(Pend[:, :], dts[:, :], AF.Exp, [dts], [Pend], scale=-C0)
            act(PCc[:, 0:8], ci[:, 63:512:64], AF.Exp, [ci], [PCc], scale=-C0)
            yield
            act(Pm[:, 0:512], ci[:, 0:512], AF.Exp, [ci, csg], [Pm], scale=-C0)
            fl["D"] = True

        def gK(hp, fl):
            kk0 = R["kk0"]
            while not fl.get("kk0"):
                yield
            nrm = R["rL"]
            act(R["sq"][:, :], kk0[:, :], AF.Square, [kk0, pcol], [R["sq"]], scale=pcol[:, PC_KK + hp:PC_KK + hp + 1])
            yield
            bs = pb[5]
            mm(bs[:, :], blk1[:, :], R["sq"][:, :], [blk1, R["sq"]], [bs])
            yield
            ts("dve", nrm[:, :], bs[:, :], 1e-24, None, ALU.max, None, [bs], [nrm])
            yield
            act(nrm[:, :], nrm[:, :], AF.Ln, [nrm], [nrm])
            yield
            act(R["beta"][:, :], nrm[:, :], AF.Exp, [nrm], [R["beta"]], scale=-0.5)
            yield
            kkn = nrm
            stt(kkn[:, :], kk0[:, :], pcol[:, PC_KK + hp:PC_KK + hp + 1], R["beta"][:, :], ALU.mult, ALU.mult, [kk0, pcol, R["beta"]], [kkn])
            yield
            while not fl.get("aa"):
                yield
            aa = R["aa"]
            beta, kmod = R["beta"], R["rN"]
            tt("dve", beta[:, :], kkn[:, :], aa[:, :], ALU.mult, [kkn, aa], [beta])
            yield
            act(kmod[:, :], aa[:, :], AF.Identity, [aa, pcol], [kmod],
                scale=pcol[:, PC_KA + hp:PC_KA + hp + 1], bias=pcol[:, PC_OMKA + hp:PC_OMKA + hp + 1])
            yield
            tt("pool", kmod[:, :], kk0[:, :], kmod[:, :], ALU.mult, [kk0, kmod], [kmod])
            fl["K"] = True

        def r1b(hp, par, fl, prevfl):
            while not (fl.get("P") and fl.get("D") and fl.get("K")):
                yield
            rr, kk0, vv = R["rr"], R["kk0"], R["vv"]
            kkn, beta, kmod = R["rL"], R["beta"], R["rN"]
            Pm, Pinv, Pex, Pend = R["rH"], R["Pinv"], R["Pex"], R["Pend"]
            bon = R["bon%d" % par]
            for hh in range(2):
                rs_ = slice(64 * hh, 64 * hh + 64)
                co = 64 * hh

                def d3(buf, rs_=rs_):
                    return buf[rs_, 0:512].rearrange("p (c t) -> p c t", t=64)

                tt("dve", BTKT3[rs_, :, co:co + 64], d3(beta), d3(Pinv), ALU.mult, [beta, Pinv], [R["BTKT"]])
                tt("dve", BTKT3[rs_, :, 128 + co:128 + co + 64], d3(kmod), d3(Pinv), ALU.mult, [kmod, Pinv], [R["BTKT"]])
                yield
                tt("pool", BH3[rs_, :, co:co + 64], d3(beta), d3(Pend), ALU.mult, [beta, Pend], [R["BH"]])
                tt("pool", KH3[rs_, :, co:co + 64], d3(kmod), d3(Pend), ALU.mult, [kmod, Pend], [R["KH"]])
                acopy(VB3[rs_, :, co:co + 64], d3(vv), [vv], [R["VB"]])
                yield
            stt(R["sq"][:, :], rr[:, :], pcol[:, PC_RK + hp:PC_RK + hp + 1], kmod[:, :], ALU.mult, ALU.mult, [rr, pcol, kmod], [R["sq"]])
            yield
            bbn = pb[5]
            mm(bbn[:, :], blk1[:, :], R["sq"][:, :], [blk1, R["sq"]], [bbn])
            yield
            tt("dve", bon[:, :], bbn[:, :], vv[:, :], ALU.mult, [bbn, vv], [bon])
            yield
            while not prevfl.get("r3done"):
                yield
            for hh in range(2):
                rs_ = slice(64 * hh, 64 * hh + 64)
                co = 64 * hh

                def d3(buf, rs_=rs_):
                    return buf[rs_, 0:512].rearrange("p (c t) -> p c t", t=64)

                stt(ALRT3[rs_, :, co:co + 64], d3(kkn), -1.0, d3(Pex), ALU.mult, ALU.mult, [kkn, Pex], [R["ALRT"]])
                tt("dve", ALRT3[rs_, :, 128 + co:128 + co + 64], d3(rr), d3(Pm), ALU.mult, [rr, Pm], [R["ALRT"]])
                yield

        bctr = [0]

        def bank8():
            bctr[0] += 1
            return pb[bctr[0] % 8]

        def evac(k, out_ap, in_ap, reads, writes):
            if k % 2 == 0:
                acopy(out_ap, in_ap, reads, writes)
            else:
                vcopy("dve", out_ap, in_ap, reads, writes)

        def r2(hp, cs):
            AbAr = [R["AbAr%d" % g_] for g_ in range(4)]
            ATg = [R["ATg%d" % g_] for g_ in range(4)]
            TRg = [R["TRg%d" % g_] for g_ in range(4)]
            QWg = [R["QWg%d" % g_] for g_ in range(4)]
            TX = [[R["TXa%d" % g_] for g_ in range(4)], [R["TXb%d" % g_] for g_ in range(4)]]
            TXx = [[R["TXax%d" % g_] for g_ in range(4)], [R["TXbx%d" % g_] for g_ in range(4)]]
            XT = [[R["XTa%d" % g_] for g_ in range(2)], [R["XTb%d" % g_] for g_ in range(2)]]
            Akr = [R["Akr%d" % g_] for g_ in range(2)]

            def s2(bufs, c, lo, hi, w=256):
                return bufs[c // 2][:, (c % 2) * w + lo:(c % 2) * w + hi]

            def s4(bufs, c, lo=0, hi=128):
                return bufs[c // 4][:, (c % 4) * 128 + lo:(c % 4) * 128 + hi]

            mu2b = mu2[:, :].unsqueeze(1).broadcast_to([128, 2, 256])
            ml2b = ml2[:, :].unsqueeze(1).broadcast_to([128, 2, 256])
            mu1b = mu2[:, 128:256].unsqueeze(1).broadcast_to([128, 4, 128])
            for g_ in range(4):
                p1 = bank8()
                for c in (2 * g_, 2 * g_ + 1):
                    o = (c % 2) * 256
                    mm(p1[:, o:o + 256], BTKT3[:, c, 0:128], ALRT3[:, c, :], [R["BTKT"], R["ALRT"]], [p1])
                tt("dve", AbAr[g_][:, :].rearrange("p (c w) -> p c w", w=256), p1[:, :].rearrange("p (c w) -> p c w", w=256), mu2b, ALU.mult, [p1, mu2], [AbAr[g_]])
                p3 = bank8()
                for c in (2 * g_, 2 * g_ + 1):
                    o = (c % 2) * 256
                    mm(p3[:, o:o + 256], ALRT3[:, c, 0:128], BTKT3[:, c, :], [R["BTKT"], R["ALRT"]], [p3])
                tt("dve", ATg[g_][:, :].rearrange("p (c w) -> p c w", w=256), p3[:, :].rearrange("p (c w) -> p c w", w=256), ml2b, ALU.mult, [p3, ml2], [ATg[g_]])
            yield
            for h_ in range(2):
                p2 = bank8()
                for c in range(4 * h_, 4 * h_ + 4):
                    o = (c % 4) * 128
                    mm(p2[:, o:o + 128], BTKT3[:, c, 128:256], ALRT3[:, c, 128:256], [R["BTKT"], R["ALRT"]], [p2])
                tt("dve", Akr[h_][:, :].rearrange("p (c w) -> p c w", w=128), p2[:, :].rearrange("p (c w) -> p c w", w=128), mu1b, ALU.mult, [p2, mu2], [Akr[h_]])
            for g_ in range(4):
                tt("pool", TX[0][g_][:, :].rearrange("p (c w) -> p c w", w=256)[:, :, 0:128],
                   AbAr[g_][:, :].rearrange("p (c w) -> p c w", w=256)[:, :, 0:128],
                   identb[:, :].unsqueeze(1).broadcast_to([128, 2, 128]), ALU.add, [AbAr[g_], identb], [TX[0][g_]])
            for h_ in range(2):
                pA = bank8()
                pB = bank8()
                for c in range(4 * h_, 4 * h_ + 4):
                    o = (c % 4) * 128
                    mm(pA[:, o:o + 128], s2(ATg, c, 0, 128), s2(AbAr, c, 0, 128), [ATg[c // 2], AbAr[c // 2]], [pA])
                for c in range(4 * h_, 4 * h_ + 4):
                    o = (c % 4) * 128
                    mm(pB[:, o:o + 128], s2(AbAr, c, 0, 128), s2(ATg, c, 0, 128), [ATg[c // 2], AbAr[c // 2]], [pB])
                for gg in (2 * h_, 2 * h_ + 1):
                    o = (gg % 2) * 256
                    evac(h_, TX[0][gg][:, :].rearrange("p (c w) -> p c w", w=256)[:, :, 128:256],
                         pA[:, o:o + 256].rearrange("p (c w) -> p c w", w=128), [pA], [TXx[0][gg]])
                evac(h_ + 1, XT[0][h_][:, :], pB[:, :], [pB], [XT[0][h_]])
            yield
            cur = 0
            for lvl in range(2, 6):
                nxt = 1 - cur
                pBs = [bank8(), bank8()]
                for g_ in range(4):
                    pA = bank8()
                    for c in (2 * g_, 2 * g_ + 1):
                        o = (c % 2) * 256
                        mm(pA[:, o:o + 128], identb[:, :], s2(TX[cur], c, 0, 128), [identb, TX[cur][g_]], [pA], start=True, stop=False, skip=True)
                        mm(pA[:, o:o + 256], s4(XT[cur], c), s2(TX[cur], c, 0, 256), [XT[cur][c // 4], TX[cur][g_], TXx[cur][g_]], [pA], start=False, stop=True, skip=True)
                    for c in (2 * g_, 2 * g_ + 1):
                        o = (c % 4) * 128
                        mm(pBs[c // 4][:, o:o + 128], s2(TX[cur], c, 128, 256), s4(XT[cur], c), [XT[cur][c // 4], TXx[cur][g_]], [pBs[c // 4]])
                    evac(g_ + lvl, TX[nxt][g_][:, :], pA[:, :], [pA], [TX[nxt][g_], TXx[nxt][g_]])
                    if g_ % 2 == 1:
                        h_ = g_ // 2
                        evac(h_ + lvl + 1, XT[nxt][h_][:, :], pBs[h_][:, :], [pBs[h_]], [XT[nxt][h_]])
                cur = nxt
                yield
            nxt = 1 - cur
            for h_ in range(2):
                pA = bank8()
                for c in range(4 * h_, 4 * h_ + 4):
                    o = (c % 4) * 128
                    mm(pA[:, o:o + 128], identb[:, :], s2(TX[cur], c, 0, 128), [identb, TX[cur][c // 2]], [pA], start=True, stop=False, skip=True)
                    mm(pA[:, o:o + 128], s4(XT[cur], c), s2(TX[cur], c, 0, 128), [XT[cur][c // 4], TX[cur][c // 2]], [pA], start=False, stop=True, skip=True)
                for gg in (2 * h_, 2 * h_ + 1):
                    o = (gg % 2) * 256
                    evac(h_ + 1, TX[nxt][gg][:, :].rearrange("p (c w) -> p c w", w=256)[:, :, 0:128],
                         pA[:, o:o + 256].rearrange("p (c w) -> p c w", w=128), [pA], [TX[nxt][gg]])
            for g_ in range(4):
                pt = bank8()
                ptb = pt[:, :].bitcast(BF16)
                for c in (2 * g_, 2 * g_ + 1):
                    for i_, src in enumerate((ALRT3[:, c, 0:128], BH3[:, c, :], KH3[:, c, :], VB3[:, c, :])):
                        o = (c % 2) * 512 + i_ * 128
                        P.op("pe", lambda e, o=o, src=src, ptb=ptb: e.transpose(out=ptb[:, o:o + 128], in_=src, identity=identb[:, :]),
                             [R["ALRT"], R["BH"], R["KH"], R["VB"], identb], [pt])
                evac(g_, TRg[g_][:, :], ptb[:, :], [pt], [TRg[g_]])
            yield
            for g_ in range(4):
                pq = bank8()
                for c in (2 * g_, 2 * g_ + 1):
                    o = (c % 2) * 256
                    Tf_ = s2(TX[nxt], c, 0, 128)
                    mm(pq[:, o:o + 128], s2(ATg, c, 128, 256), Tf_, [ATg[g_], TX[nxt][g_]], [pq])
                    mm(pq[:, o + 128:o + 256], s2(TRg, c, 0, 128, w=512), Tf_, [TRg[g_], TX[nxt][g_]], [pq])
                evac(g_ + 1, QWg[g_][:, :], pq[:, :], [pq], [QWg[g_]])
            yield

        def r3(hp, cs, par, myfl):
            Ut = R["Ut"]
            PCc = R["PCc%d" % par]
            for c in cs:
                AbArB, AkrB, TRB, QWB = R["AbAr%d" % (c // 2)], R["Akr%d" % (c // 4)], R["TRg%d" % (c // 2)], R["QWg%d" % (c // 2)]
                AbAr = Buf(AbArB[:, (c % 2) * 256:(c % 2) * 256 + 256], "v")
                Akr = Buf(AkrB[:, (c % 4) * 128:(c % 4) * 128 + 128], "v")
                TR = Buf(TRB[:, (c % 2) * 512:(c % 2) * 512 + 512], "v")
                QW = Buf(QWB[:, (c % 2) * 256:(c % 2) * 256 + 256], "v")
                pu = rbank()
                mm(pu[:, 0:128], QW[:, 0:128], TR[:, 384:512], [QWB, TRB], [pu], start=True, stop=False)
                mm(pu[:, 0:128], QW[:, 128:256], Sb[hp][:, :], [QWB, Sb[hp]], [pu], start=False, stop=True)
                acopy(Ut[:, :], pu[:, 0:128], [pu], [Ut])
                yield
                py = rbank()
                mm(py[:, 0:128], Sb[hp][:, :], ALRT3[:, c, 128:256], [Sb[hp], R["ALRT"]], [py], start=True, stop=False)
                mm(py[:, 0:128], TR[:, 384:512], Akr[:, :], [TRB, AkrB], [py], start=False, stop=False)
                mm(py[:, 0:128], Ut[:, :], AbAr[:, 128:256], [Ut, AbArB], [py], start=False, stop=True)
                pS = rbank()
                mm(pS[:, 0:128], TR[:, 256:384], TR[:, 384:512], [TRB], [pS], start=True, stop=False)
                mm(pS[:, 0:128], TR[:, 128:256], Ut[:, :], [TRB, Ut], [pS], start=False, stop=True)
                if c == cs[-1]:
                    myfl["r3done"] = True
                yield
                stt(Sb[hp][:, :], Sf[hp][:, :], PCc[:, c:c + 1], pS[:, 0:128], ALU.mult, ALU.add, [Sf[hp], PCc, pS], [Sb[hp]])
                stt(Sf[hp][:, :], Sf[hp][:, :], PCc[:, c:c + 1], pS[:, 0:128], ALU.mult, ALU.add, [Sf[hp], PCc, pS], [Sf[hp]])
                acopy(R["Yb"][0:64, c * 64:(c + 1) * 64], py[0:64, 0:64], [py], [R["Yb"]])
                acopy(R["Yb"][64:128, c * 64:(c + 1) * 64], py[64:128, 64:128], [py], [R["Yb"]])
                yield

        def r4(hp, tok0, par):
            Yb, yc = R["Yb"], R["yc"]
            bon = R["bon%d" % par]
            acopy(R["Ybf"][:, :], Yb[:, :], [Yb], [R["Ybf"]])
            slot = get_w(53 + hp)
            gbank = rbank()
            proj_fm(slot, tok0, 512, gbank)
            pm = rbank()
            mm(pm[:, :], blkavg[:, :], R["Ybf"][:, :], [blkavg, R["Ybf"]], [pm])
            yield
            act(R["zr"][:, :], gbank[:, :], AF.Silu, [gbank], [R["zr"]])
            tt("dve", yc[:, :], Yb[:, :], pm[:, :], ALU.subtract, [Yb, pm], [yc])
            act(R["Ybf"][:, :], yc[:, :], AF.Square, [yc], [R["Ybf"]])
            pv = rbank()
            mm(pv[:, :], blkavg[:, :], R["Ybf"][:, :], [blkavg, R["Ybf"]], [pv])
            yield
            rstd = Yb
            act(rstd[:, :], pv[:, :], AF.Ln, [pv], [rstd], bias=GN_EPS, scale=1.0)
            act(rstd[:, :], rstd[:, :], AF.Exp, [rstd], [rstd], scale=-0.5)
            tt("dve", yc[:, :], yc[:, :], rstd[:, :], ALU.mult, [yc, rstd], [yc])
            yield
            act(yc[:, :], yc[:, :], AF.Identity, [yc, pcol], [yc],
                scale=pcol[:, PC_LNW + hp:PC_LNW + hp + 1], bias=pcol[:, PC_LNB + hp:PC_LNB + hp + 1])
            tt("dve", yc[:, :], yc[:, :], bon[:, :], ALU.add, [yc, bon], [yc])
            tt("dve", orT[hp][:, tok0:tok0 + 512], yc[:, :], R["zr"][:, :], ALU.mult, [yc, R["zr"]], [orT[hp]])
            yield

        def chain(*gens):
            for g_ in gens:
                yield from g_

        def lab(name, g_):
            while True:
                P.phase = name
                try:
                    next(g_)
                except StopIteration:
                    return
                yield

        SEQM = ""
        R1W = 4

        def interleave(gens, tag=""):
            gens = list(gens)
            if tag in SEQM or "1" in SEQM:
                for g_ in gens:
                    for _ in g_:
                        pass
                return
            while gens:
                for g_ in list(gens):
                    try:
                        next(g_)
                    except StopIteration:
                        gens.remove(g_)

        units = [(blk, hp) for blk in range(4) for hp in range(4)]

        def interleave_gen(gens):
            gens = list(gens)
            while gens:
                for g_ in list(gens):
                    try:
                        next(g_)
                        yield
                    except StopIteration:
                        gens.remove(g_)

        def take(g_, n):
            done = False
            while not done:
                for _ in range(n):
                    try:
                        next(g_)
                    except StopIteration:
                        done = True
                        break
                if not done:
                    yield

        ufl = [dict() for _ in units]

        def mk_r1(u):
            blk, hp = units[u]
            fl = {}
            par = u % 2
            prevfl = ufl[u - 1] if u > 0 else {"r3done": True}
            return [lab("r1a", gP(hp, blk * 512, hp == 0, fl)), lab("r1a", gD(hp, par, fl)), lab("r1a", gK(hp, fl)),
                    lab("r1b", r1b(hp, par, fl, prevfl))]

        interleave(mk_r1(0))
        ALLC = [0, 1, 2, 3, 4, 5, 6, 7]
        for u, (blk, hp) in enumerate(units):
            par = u % 2
            tok0 = blk * 512
            interleave([lab("r2", r2(hp, ALLC))], "a")
            tail = chain(lab("r3", r3(hp, ALLC, par, ufl[u])), lab("r4", r4(hp, tok0, par)))
            if u + 1 < len(units):
                interleave([tail, take(interleave_gen(mk_r1(u + 1)), R1W)], "c")
            else:
                interleave([tail])

        P.phase = "F%d" % s
        ph = new_phase([("sga", 512, F32), ("sgb", 512, F32), ("f1", 512, F32), ("f2", 512, F32), ("mT", 8 * 512, BF16),
                        ("ysb0", 1024, F32), ("ysb1", 1024, F32), ("xr0", 1024, F32), ("xr1", 1024, F32), ("fj", 512, BF16), ("pgain", 1024, F32)])
        pgain = ph["pgain"]
        P.dma("pool", pgain[:, :], pg_d[0:1, :].broadcast_to([128, D]), reads=[pg_d], writes=[pgain], sembuf=pgain)
        sga, sgb, f1, f2, mT, fj = ph["sga"], ph["sgb"], ph["f1"], ph["f2"], ph["mT"], ph["fj"]
        ysb = [ph["ysb0"], ph["ysb1"]]
        xr = [ph["xr0"], ph["xr1"]]
        mT3 = mT[:, :].rearrange("p (k t) -> p k t", t=512)
        fctr = [0]

        def fbank():
            fctr[0] += 1
            return pb[fctr[0] % 6]

        for blk in range(4):
            tok0 = blk * 512
            for dt in range(8):
                slot = get_w(57 + dt)
                bank = pbank()
                proj_fm(slot, tok0, 512, bank)
                act(sga[:, :], bank[:, :], AF.Sigmoid, [bank], [sga])
                slot = get_w(65 + dt)
                bank = pbank()
                proj_fm(slot, tok0, 512, bank)
                act(sgb[:, :], bank[:, :], AF.Sigmoid, [bank], [sgb])
                ya = fbank()
                for fc in range(4):
                    mm(ya[:, :], wua3[:, fc, dt * 128:(dt + 1) * 128], oaT[fc][:, tok0:tok0 + 512], [wua, oaT[fc]], [ya], start=(fc == 0), stop=(fc == 3))
                yb_ = fbank()
                for fc in range(4):
                    mm(yb_[:, :], wur3[:, fc, dt * 128:(dt + 1) * 128], orT[fc][:, tok0:tok0 + 512], [wur, orT[fc]], [yb_], start=(fc == 0), stop=(fc == 3))
                tt("dve", f1[:, :], ya[:, :], sga[:, :], ALU.mult, [ya, sga], [f1])
                tt("dve", f2[:, :], yb_[:, :], sgb[:, :], ALU.mult, [yb_, sgb], [f2])
                tt("pool", mT3[:, dt, :], f1[:, :], f2[:, :], ALU.add, [f1, f2], [mT])
            for t4 in range(4):
                ti = blk * 4 + t4
                yt = ysb[ti % 2]
                xt = xr[ti % 2]
                P.dma("pool", xt[:, :], x_d[s, ti * 128:(ti + 1) * 128, :], reads=[x_d], writes=[xt], sembuf=xt)
                banks = [fbank(), fbank()]
                for hf in range(2):
                    for dc in range(8):
                        mm(banks[hf][:, :], mT3[:, dc, t4 * 128:(t4 + 1) * 128], wout3[:, dc, hf * 512:(hf + 1) * 512], [mT, wout], [banks[hf]],
                           start=(dc == 0), stop=(dc == 7))
                    act(fj[:, :], banks[hf][:, :], AF.Square, [banks[hf]], [fj, small], accum_out=small[:, 4 + hf:5 + hf])
                tt("dve", small[:, 6:7], small[:, 4:5], small[:, 5:6], ALU.add, [small], [small])
                act(small[:, 7:8], small[:, 6:7], AF.Sqrt, [small], [small], scale=1.0 / D, bias=RMS_EPS)
                recip(small[:, 8:9], small[:, 7:8], [small], [small])
                for hf in range(2):
                    stt(yt[:, hf * 512:(hf + 1) * 512], banks[hf][:, :], small[:, 8:9], pgain[:, hf * 512:(hf + 1) * 512], ALU.mult, ALU.mult,
                        [banks[hf], small, pgain], [yt])
                tt("pool", yt[:, :], yt[:, :], xt[:, :], ALU.add, [yt, xt], [yt])
                tok = P.dma("pool", out_d[s, ti * 128:(ti + 1) * 128, :], yt[:, :], reads=[yt], writes=[out_d], sembuf=yt)
                out_toks.append(tok)

    P.wait_all("pool", out_toks)
    P.emit()
    P.close()
    nc._labels = P.labels
    return nc


_CACHE = {}


def _host_inputs(inp):
    f = lambda a: np.ascontiguousarray(np.asarray(a, dtype=np.float32))
    pcol = np.zeros((128, 80), np.float32)
    pcol[:, 0:8] = f(inp["pre_norm_gain"])[0].reshape(8, 128).T
    pcol[:, 8:21] = f(inp["rwkv_shift_mix"])[0].reshape(13, 128).T
    for off, key in ((21, "rwkv_w0"), (25, "rwkv_a0"), (29, "rwkv_k_k"), (33, "rwkv_k_a"), (37, "rwkv_r_k"),
                     (41, "rwkv_ln_w"), (45, "rwkv_ln_b")):
        pcol[:, off:off + 4] = f(inp[key])[0].reshape(4, 128).T
    c = _const_tables()
    cst = np.concatenate([c["ident"], c["blk1"], c["mu2"], c["ml2"]], axis=1).astype(np.float32)
    shared = {
        "w_in": f(inp["w_in"])[0],
        "biasm": _bias_tables(f(inp["rel_bias"])),
        "pcol": pcol,
        "pgain": f(inp["post_norm_gain"])[0].reshape(1, D),
        "lora": np.concatenate([f(inp["rwkv_w_up"])[0], f(inp["rwkv_a_up"])[0]], axis=0),
        "wua": f(inp["w_up_attn"])[0],
        "wur": f(inp["w_up_rwkv"])[0],
        "wout": f(inp["w_out"])[0],
        "cst": cst,
    }
    return shared


def kernel(**inputs):
    x = np.ascontiguousarray(np.asarray(inputs["x"], dtype=np.float32))
    shared = _host_inputs(inputs)
    if "nc" not in _CACHE:
        _CACHE["nc"] = build(NSEQ)
    nc = _CACHE["nc"]
    in_maps = []
    for c in range(NCORES):
        m = dict(shared)
        m["x"] = x[c * NSEQ:(c + 1) * NSEQ]
        in_maps.append(m)
    res = run_bass_kernel_spmd(nc, in_maps, core_ids=list(range(NCORES)))
    return np.concatenate([r["out"] for r in res.results], axis=0)
```

```python
from contextlib import ExitStack
import math
import numpy as np
import concourse.bass as bass
import concourse.mybir as mybir
from concourse.bass_utils import run_bass_kernel_spmd

F32 = mybir.dt.float32
BF16 = mybir.dt.bfloat16
ALU = mybir.AluOpType
AF = mybir.ActivationFunctionType
ENGS = ("pe", "act", "dve", "pool", "sp")

NCORES = 8
NSEQ = 4
S = 2048
D = 1024
NTILE = 73
C0 = math.exp(-0.5)
RMS_EPS = 1e-6
GN_EPS = 64e-5
NEG = -30000.0


class Buf:
    def __init__(self, t, name):
        self.t = t
        self.name = name
        self.last_write = None
        self.readers = {}
        self.dsem = None

    def __getitem__(self, idx):
        return self.t[idx]


class Prog:
    def __init__(self, nc):
        self.nc = nc
        self.stack = ExitStack()
        self.q = {e: [] for e in ENGS}
        self.cnt = {}
        self.waited = {e: {} for e in ENGS}
        self.sems = {}
        self.nbuf = 0
        self.dsems = {}
        self.phase = "setup"
        self.labels = {e: [] for e in ENGS}
        for e in ENGS:
            self._sem("E_" + e)

    def _sem(self, key):
        if key not in self.sems:
            self.sems[key] = self.stack.enter_context(self.nc.semaphore("s%d" % len(self.sems)))
            self.cnt[key] = 0
        return key

    def sbuf(self, name, shape, dtype):
        t = self.stack.enter_context(self.nc.sbuf_tensor(name, list(shape), dtype))
        return Buf(t, name)

    def psum(self, name, shape, dtype):
        t = self.stack.enter_context(self.nc.psum_tensor(name, list(shape), dtype))
        b = Buf(t, name)
        b.is_psum = True
        return b

    def dram(self, name, shape, dtype, kind):
        t = self.nc.dram_tensor(name, list(shape), dtype, kind=kind).ap()
        return Buf(t, name)

    def view(self, ap, name):
        return Buf(ap, name)

    def _deps(self, eng, reads, writes, is_dma):
        own = "E_" + eng
        deps = {}

        def add(k, v):
            if deps.get(k, 0) < v:
                deps[k] = v

        for b in reads:
            if b.last_write is not None:
                k, v = b.last_write
                if not (k == own and eng == "pe"):
                    add(k, v)
            if getattr(b, "is_psum", False):
                for k, v in b.readers.items():
                    if k != own:
                        add(k, v)
        for b in writes:
            if b.last_write is not None:
                k, v = b.last_write
                if not (k == own and eng == "pe"):
                    add(k, v)
            for k, v in b.readers.items():
                if k == own and eng == "pe":
                    continue
                add(k, v)
        w = self.waited[eng]
        out = []
        for k, v in deps.items():
            if w.get(k, 0) < v:
                w[k] = v
                out.append((k, v))
        return out

    def _commit(self, tok, reads, writes):
        k, v = tok
        for b in reads:
            if b.readers.get(k, 0) < v:
                b.readers[k] = v
        for b in writes:
            b.last_write = tok
            b.readers = {}

    def op(self, eng, fn, reads=(), writes=()):
        reads = [b for b in reads if b is not None]
        writes = [b for b in writes if b is not None]
        waits = self._deps(eng, reads, writes, False)
        key = "E_" + eng
        self.cnt[key] += 1
        tok = (key, self.cnt[key])
        self.q[eng].append((waits, fn, key, 1))
        self.labels[eng].append(self.phase)
        self._commit(tok, reads, writes)
        return tok

    def dma(self, eng, out_ap, in_ap, reads=(), writes=(), sembuf=None):
        reads = [b for b in reads if b is not None]
        writes = [b for b in writes if b is not None]
        waits = self._deps(eng, reads, writes, True)
        sb = sembuf
        nk = (sb.name, eng)
        if nk not in self.dsems:
            self.dsems[nk] = self._sem("D_%d" % self.nbuf)
            self.nbuf += 1
        key = self.dsems[nk]
        self.cnt[key] += 16
        tok = (key, self.cnt[key])

        def fn(e, out_ap=out_ap, in_ap=in_ap):
            return e.dma_start(out=out_ap, in_=in_ap)

        self.q[eng].append((waits, fn, key, 16))
        self._commit(tok, reads, writes)
        return tok

    def alias(self, olds, news):
        merged = {}
        for b in olds:
            if b.last_write is not None:
                k, v = b.last_write
                merged[k] = max(merged.get(k, 0), v)
            for k, v in b.readers.items():
                merged[k] = max(merged.get(k, 0), v)
        for b in news:
            for k, v in merged.items():
                if b.readers.get(k, 0) < v:
                    b.readers[k] = v

    def wait_all(self, eng, toks):
        w = self.waited[eng]
        waits = []
        for k, v in toks:
            if w.get(k, 0) < v:
                w[k] = v
                waits.append((k, v))
        self.q[eng].append((waits, None, None, 0))

    def emit(self):
        nc = self.nc
        sems = self.sems
        qs = self.q
        with nc.Block() as block:
            def run(eng_name):
                def body(e):
                    for waits, fn, key, inc in qs[eng_name]:
                        for k, v in waits:
                            e.wait_ge(sems[k], v)
                        if fn is not None:
                            fn(e).then_inc(sems[key], inc)
                return body

            block.tensor(run("pe"))
            block.scalar(run("act"))
            block.vector(run("dve"))
            block.gpsimd(run("pool"))
            block.sync(run("sp"))

    def close(self):
        self.stack.close()


def _t5_bucket(dist):
    d = np.maximum(np.asarray(dist), 0)
    max_exact = 16
    ratio = np.log(np.maximum(d, 1).astype(np.float32) / max_exact) / np.float32(math.log(2048 / max_exact))
    large = max_exact + (ratio * (32 - max_exact)).astype(np.int32)
    large = np.minimum(large, 31)
    return np.where(d < max_exact, d, large).astype(np.int32)


def _bias_tables(rel_bias):
    lk = np.arange(128)[:, None]
    lq = np.arange(256)[None, :]
    rel = lq - lk
    valid = (rel >= 0) & (rel <= 128)
    out = np.empty((24, 128, 256), np.float32)
    for g, dil in enumerate((1, 4, 16)):
        bucket = _t5_bucket(np.maximum(rel, 0) * dil)
        for h in range(8):
            tab = rel_bias[:, g * 8 + h][bucket]
            out[g * 8 + h] = np.where(valid, tab, np.float32(NEG))
    return out


def _const_tables():
    c = {}
    c["ident"] = np.eye(128, dtype=np.float32)
    blk = np.zeros((128, 128), np.float32)
    blk[:64, :64] = 1.0
    blk[64:, 64:] = 1.0
    c["blk1"] = blk
    j = np.arange(128)[:, None]
    t = np.arange(128)[None, :]
    same = (j // 64) == (t // 64)
    su = (same & (j < t)).astype(np.float32)
    iu = (same & (j <= t)).astype(np.float32)
    sl = (same & (j > t)).astype(np.float32)
    c["mu2"] = np.concatenate([su, iu], axis=1)
    c["ml2"] = np.concatenate([sl, sl], axis=1)
    return c


def build(nseq=NSEQ):
    nc = bass.Bass("TRN2", target_bir_lowering=False)
    P = Prog(nc)
    x_d = P.dram("x", [nseq, S, D], F32, "ExternalInput")
    win_d = P.dram("w_in", [D, NTILE * 128], F32, "ExternalInput")
    bias_d = P.dram("biasm", [24, 128, 256], F32, "ExternalInput")
    pcol_d = P.dram("pcol", [128, 80], F32, "ExternalInput")
    pg_d = P.dram("pgain", [1, D], F32, "ExternalInput")
    lora_d = P.dram("lora", [128, 512], F32, "ExternalInput")
    wua_d = P.dram("wua", [512, D], F32, "ExternalInput")
    wur_d = P.dram("wur", [512, D], F32, "ExternalInput")
    wout_d = P.dram("wout", [D, D], F32, "ExternalInput")
    cst_d = P.dram("cst", [128, 768], F32, "ExternalInput")
    out_d = P.dram("out", [nseq, S, D], F32, "ExternalOutput")
    wbf_d = [P.dram("wbf%d" % i, [128, 1024], BF16, "Internal") for i in range(NTILE)]

    hT = P.sbuf("hT", [128, 8 * S], BF16)
    hT3 = hT[:, :].rearrange("p (k t) -> p k t", t=S)
    NSLOT = 3
    wsl = [P.sbuf("wsl%d" % i, [128, 1024], BF16) for i in range(NSLOT)]
    wua = P.sbuf("wua_s", [128, 4 * D], BF16)
    wur = P.sbuf("wur_s", [128, 4 * D], BF16)
    wout = P.sbuf("wout_s", [128, 8 * D], BF16)
    lora = P.sbuf("lora_s", [128, 512], BF16)
    pcol = P.sbuf("pcol_s", [128, 80], F32)
    cstf = P.sbuf("cstf", [128, 768], F32)
    identb = P.sbuf("identb", [128, 128], BF16)
    blk1 = P.sbuf("blk1b", [128, 128], BF16)
    blkavg = P.sbuf("blkavg", [128, 128], BF16)
    onesf = P.sbuf("onesf", [128, 64], F32)
    ones5 = P.sbuf("ones5", [128, 512], BF16)
    oaT = [P.sbuf("oaT%d" % i, [128, S], BF16) for i in range(4)]
    orT = [P.sbuf("orT%d" % i, [128, S], BF16) for i in range(4)]
    Sf = [P.sbuf("Sf%d" % i, [128, 128], F32) for i in range(4)]
    Sb = [P.sbuf("Sb%d" % i, [128, 128], BF16) for i in range(4)]
    prevc = P.sbuf("prevc", [128, 16], F32)
    small = P.sbuf("small", [128, 16], F32)
    U = P.sbuf("U", [128, 23552], F32)

    pb = [P.psum("pb%d" % i, [128, 512], F32) for i in range(8)]

    PC_GAIN, PC_MIX, PC_W0, PC_A0, PC_KK, PC_KA, PC_RK, PC_LNW, PC_LNB, PC_OMKA = 0, 8, 21, 25, 29, 33, 37, 41, 45, 49
    PC_OMIX = 56

    def carve(specs):
        off = 0
        res = {}
        for name, ncols, dt in specs:
            words = ncols if dt == F32 else (ncols + 1) // 2
            ap = U[:, off:off + words]
            if dt == BF16:
                ap = ap.bitcast(BF16)
            res[name] = Buf(ap, name)
            off += words
        assert off <= 23552, off
        return res

    state = {"phase": []}

    def new_phase(specs):
        bufs = carve(specs)
        P.alias(state["phase"] + [U], list(bufs.values()))
        state["phase"] = list(bufs.values())
        return bufs

    def mm(out_ap, lhsT, rhs, reads, writes, start=True, stop=True, skip=False):
        P.op("pe", lambda e: e.matmul(out_ap, lhsT=lhsT, rhs=rhs, start=start, stop=stop, skip_group_check=skip), reads, writes)

    def act(out_ap, in_ap, func, reads, writes, **kw):
        P.op("act", lambda e: e.activation(out=out_ap, in_=in_ap, func=func, **kw), reads, writes)

    def acopy(out_ap, in_ap, reads, writes):
        P.op("act", lambda e: e.copy(out=out_ap, in_=in_ap), reads, writes)

    def tt(eng, out_ap, in0, in1, op, reads, writes):
        P.op(eng, lambda e: e.tensor_tensor(out=out_ap, in0=in0, in1=in1, op=op), reads, writes)

    def ts(eng, out_ap, in0, s1, s2, op0, op1, reads, writes):
        if op1 is None:
            P.op(eng, lambda e: e.tensor_scalar(out=out_ap, in0=in0, scalar1=s1, scalar2=None, op0=op0), reads, writes)
        else:
            P.op(eng, lambda e: e.tensor_scalar(out=out_ap, in0=in0, scalar1=s1, scalar2=s2, op0=op0, op1=op1), reads, writes)

    def stt(out_ap, in0, scalar, in1, op0, op1, reads, writes):
        P.op("dve", lambda e: e.scalar_tensor_tensor(out=out_ap, in0=in0, scalar=scalar, in1=in1, op0=op0, op1=op1), reads, writes)

    def vcopy(eng, out_ap, in_ap, reads, writes):
        P.op(eng, lambda e: e.tensor_copy(out=out_ap, in_=in_ap), reads, writes)

    def recip(out_ap, in_ap, reads, writes):
        P.op("dve", lambda e: e.reciprocal(out=out_ap, in_=in_ap), reads, writes)

    wctr = [0]

    def get_w(tile):
        slot = wsl[wctr[0] % NSLOT]
        wctr[0] += 1
        P.dma("sp", slot[:, :], wbf_d[tile][:, :], reads=[wbf_d[tile]], writes=[slot], sembuf=slot)
        return slot

    def proj_fm(slot, tok0, n, bank, m0=0, m1=128):
        w3 = slot[:, :].rearrange("p (k c) -> p k c", c=128)
        for kc in range(8):
            mm(bank[0:m1 - m0, 0:n], w3[:, kc, m0:m1], hT3[:, kc, tok0:tok0 + n], [slot, hT], [bank],
               start=(kc == 0), stop=(kc == 7))

    P.dma("pool", pcol[:, :], pcol_d[:, :], reads=[pcol_d], writes=[pcol], sembuf=pcol)
    P.dma("pool", cstf[:, :], cst_d[:, :], reads=[cst_d], writes=[cstf], sembuf=cstf)
    vcopy("dve", identb[:, :], cstf[:, 0:128], [cstf], [identb])
    vcopy("dve", blk1[:, :], cstf[:, 128:256], [cstf], [blk1])
    ts("dve", blkavg[:, :], cstf[:, 128:256], 1.0 / 64.0, None, ALU.mult, None, [cstf], [blkavg])
    P.op("dve", lambda e: e.memset(onesf[:, :], 1.0), [], [onesf])
    P.op("dve", lambda e: e.memset(ones5[:, :], 1.0), [], [ones5])
    ts("dve", pcol[:, PC_OMKA:PC_OMKA + 4], pcol[:, PC_KA:PC_KA + 4], -1.0, 1.0, ALU.mult, ALU.add, [pcol], [pcol])
    ts("dve", pcol[:, PC_OMIX:PC_OMIX + 13], pcol[:, PC_MIX:PC_MIX + 13], -1.0, 1.0, ALU.mult, ALU.add, [pcol], [pcol])

    ph = new_phase([("stg0", 1024, F32), ("stg1", 1024, F32), ("cb0", 1024, BF16), ("cb1", 1024, BF16)])
    stg = [ph["stg0"], ph["stg1"]]
    cb = [ph["cb0"], ph["cb1"]]
    def load_resident(dst, src_d, nchunk, ncols):
        for kc in range(nchunk):
            for hf in range(ncols // 1024):
                s_ = stg[(kc + hf) % 2]
                P.dma("sp", s_[:, :], src_d[kc * 128:(kc + 1) * 128, hf * 1024:(hf + 1) * 1024], reads=[src_d], writes=[s_], sembuf=s_)
                vcopy("dve", dst[:, kc * ncols + hf * 1024: kc * ncols + (hf + 1) * 1024], s_[:, :], [s_], [dst])

    load_resident(wua, wua_d, 4, D)
    load_resident(wur, wur_d, 4, D)
    load_resident(wout, wout_d, 8, D)
    P.dma("sp", stg[0][:, 0:512], lora_d[:, :], reads=[lora_d], writes=[stg[0]], sembuf=stg[0])
    vcopy("dve", lora[:, :], stg[0][:, 0:512], [stg[0]], [lora])
    wua3 = wua[:, :].rearrange("p (k c) -> p k c", c=D)
    wur3 = wur[:, :].rearrange("p (k c) -> p k c", c=D)
    wout3 = wout[:, :].rearrange("p (k c) -> p k c", c=D)

    out_toks = []

    for s in range(nseq):
        P.phase = "X%d" % s
        ph = new_phase([("xb0", 1024, F32), ("xb1", 1024, F32), ("xs", 1024, BF16), ("junk", 1024, BF16)])
        xb = [ph["xb0"], ph["xb1"]]
        xs, junk = ph["xs"], ph["junk"]
        for tt_ in range(16):
            xt = xb[tt_ % 2]
            P.dma("pool", xt[:, :], x_d[s, tt_ * 128:(tt_ + 1) * 128, :], reads=[x_d], writes=[xt], sembuf=xt)
            act(junk[:, :], xt[:, :], AF.Square, [xt], [junk, small], accum_out=small[:, 0:1])
            act(small[:, 1:2], small[:, 0:1], AF.Sqrt, [small], [small], scale=1.0 / D, bias=RMS_EPS)
            recip(small[:, 2:3], small[:, 1:2], [small], [small])
            ts("dve", xs[:, :], xt[:, :], small[:, 2:3], None, ALU.mult, None, [xt, small], [xs])
            bank = pb[6 + tt_ % 2]
            pbf = bank[:, :].bitcast(BF16)
            for kc in range(8):
                P.op("pe", lambda e, kc=kc, pbf=pbf: e.transpose(out=pbf[:, kc * 128:(kc + 1) * 128], in_=xs[:, kc * 128:(kc + 1) * 128], identity=identb[:, :]),
                     [xs, identb], [bank])
            tt("dve", hT3[:, :, tt_ * 128:(tt_ + 1) * 128], pbf.rearrange("p (k t) -> p k t", t=128),
               pcol[:, PC_GAIN:PC_GAIN + 8].unsqueeze(2).broadcast_to([128, 8, 128]), ALU.mult, [bank, pcol], [hT])

        if s == 0:
            order = []
            for hp_ in range(4):
                for g_ in range(3):
                    order += [(w_ * 3 + g_) * 4 + hp_ for w_ in range(3)]
                order.append(36 + hp_)
            order += [t_ for t_ in range(NTILE) if t_ not in order]
            for i_, t in enumerate(order):
                src = win_d[:, t * 128:(t + 1) * 128].rearrange("(k p) j -> p k j", p=128)
                lead = order[i_ - i_ % 2]
                P.dma("pool", wbf_d[t][:, :].rearrange("p (k j) -> p k j", j=128), src, reads=[win_d], writes=[wbf_d[t]], sembuf=wbf_d[lead])
                if i_ % 2 == 1:
                    wbf_d[lead].last_write = wbf_d[t].last_write

        P.phase = "A%d" % s
        specs = []
        for g in range(3):
            specs += [("qT%d" % g, S, BF16), ("kT%d" % g, S, BF16), ("vb%d" % g, 16 * 192, BF16)]
        specs += [("gz", S, BF16), ("bias", 6 * 256, F32), ("tmp0", 256, F32), ("tmp1", 256, F32), ("tmp2", 256, F32), ("tmp3", 256, F32),
                  ("pT0", 256, BF16), ("pT1", 256, BF16), ("pT2", 256, BF16), ("pT3", 256, BF16), ("rden", 512, F32), ("bcs", 512, F32), ("t1", 512, F32), ("t1b", 512, F32), ("bcsb", 512, F32)]
        ph = new_phase(specs)
        qT = [ph["qT%d" % g] for g in range(3)]
        kT = [ph["kT%d" % g] for g in range(3)]
        vb = [ph["vb%d" % g] for g in range(3)]
        gz, biasb, rden, bcs, t1 = ph["gz"], ph["bias"], ph["rden"], ph["bcs"], ph["t1"]
        tmpb = [ph["tmp%d" % i] for i in range(4)]
        pTb = [ph["pT%d" % i] for i in range(4)]
        for g in range(3):
            v3 = vb[g][:, :].rearrange("p (b c) -> p b c", c=192)
            P.op("pool", lambda e, v3=v3: e.memset(v3[:, :, 64:65], 1.0), [], [vb[g]])
            P.op("pool", lambda e, v3=v3: e.memset(v3[:, :, 65:128], 0.0), [], [vb[g]])
        pctr = [0]

        def pbank():
            pctr[0] += 1
            return pb[6 + pctr[0] % 2]

        for hp in range(4):
            for g in range(3):
                for hh in range(2):
                    i6 = g * 2 + hh
                    P.dma("pool", biasb[:, i6 * 256:(i6 + 1) * 256], bias_d[g * 8 + hp * 2 + hh, :, :], reads=[bias_d], writes=[biasb], sembuf=biasb)
            for g, dil in enumerate((1, 4, 16)):
                L = S // dil
                for which, dst in ((0, qT[g]), (1, kT[g])):
                    slot = get_w((which * 3 + g) * 4 + hp)
                    for b4 in range(4):
                        bank = pbank()
                        proj_fm(slot, b4 * 512, 512, bank)
                        if dil == 1:
                            acopy(dst[:, b4 * 512:(b4 + 1) * 512], bank[:, :], [bank], [dst])
                        else:
                            n = 512 // dil
                            o3 = dst[:, :].rearrange("p (r l) -> p r l", r=dil)[:, :, b4 * n:(b4 + 1) * n]
                            i3 = bank[:, :].rearrange("p (l r) -> p r l", r=dil)
                            acopy(o3, i3, [bank], [dst])
                slot = get_w((6 + g) * 4 + hp)
                w3 = slot[:, :].rearrange("p (k c) -> p k c", c=128)
                v3 = vb[g][:, :].rearrange("p (b c) -> p b c", c=192)
                nbj = 16 // dil
                for q4 in range(4):
                    bank = pbank()
                    for i4 in range(4):
                        kb = q4 * 4 + i4
                        rho, j = kb // nbj, kb % nbj
                        t0 = rho + dil * 128 * j
                        for kc in range(8):
                            mm(bank[:, i4 * 128:(i4 + 1) * 128], hT3[:, kc, t0:t0 + dil * 127 + 1:dil], w3[:, kc, :], [hT, slot], [bank],
                               start=(kc == 0), stop=(kc == 7))
                    b3 = bank[:, :].rearrange("p (b c) -> p b c", c=128)
                    acopy(v3[:, q4 * 4:(q4 + 1) * 4, 0:64], b3[:, :, 0:64], [bank], [vb[g]])
                    vcopy("dve", v3[:, q4 * 4:(q4 + 1) * 4, 128:192], b3[:, :, 64:128], [bank], [vb[g]])
            slot = get_w(36 + hp)
            for b4 in range(4):
                bank = pbank()
                proj_fm(slot, b4 * 512, 512, bank)
                act(gz[:, b4 * 512:(b4 + 1) * 512], bank[:, :], AF.Silu, [bank], [gz])

            for hh in range(2):
                r0 = 64 * hh
                hrows = slice(r0, r0 + 64)
                if hh == 0:
                    orows, lcols, drow = slice(0, 65), slice(0, 65), 64
                else:
                    orows, lcols, drow = slice(0, 128), slice(64, 192), 0
                nrows = slice(r0, r0 + 64)
                started = [False] * 4
                plan = []
                for g, dil in ((2, 16), (1, 4), (0, 1)):
                    L = S // dil
                    nbj = 16 // dil
                    for kb in range(16):
                        rho, j = kb // nbj, kb % nbj
                        nq = 256 if j + 1 < nbj else 128
                        avs = []
                        for m_ in range(nq // 128):
                            jq = j + m_
                            if dil == 1:
                                avs.append((jq // 4, slice((jq % 4) * 128, (jq % 4) * 128 + 128), slice(m_ * 128, m_ * 128 + 128)))
                            elif dil == 4:
                                avs.append((jq, slice(rho, rho + 4 * 127 + 1, 4), slice(m_ * 128, m_ * 128 + 128)))
                            else:
                                for b_ in range(4):
                                    avs.append((b_, slice(rho, rho + 16 * 31 + 1, 16), slice(b_ * 32, b_ * 32 + 32)))
                        plan.append((g, kb, rho, j, nq, avs))
                last = {}
                for pi, (g, kb, rho, j, nq, avs) in enumerate(plan):
                    for ai, a_ in enumerate(avs):
                        last[a_[0]] = (pi, ai)
                SKEW = 3

                def emit_scores(pi):
                    g, kb, rho, j, nq, avs = plan[pi]
                    dil = (1, 4, 16)[g]
                    L = S // dil
                    sb_ = pb[4 + pi % 4]
                    tm = tmpb[pi % 4]
                    pT = pTb[pi % 4]
                    kcol = rho * L + j * 128
                    mm(sb_[:, 0:nq], kT[g][hrows, kcol:kcol + 128], qT[g][hrows, kcol:kcol + nq], [kT[g], qT[g]], [sb_])
                    i6 = g * 2 + hh
                    stt(tm[:, 0:nq], sb_[:, 0:nq], 0.125, biasb[:, i6 * 256:i6 * 256 + nq], ALU.mult, ALU.add, [sb_, biasb], [tm])
                    act(pT[:, 0:nq], tm[:, 0:nq], AF.Exp, [tm], [pT])

                def emit_av(pi):
                    g, kb, rho, j, nq, avs = plan[pi]
                    pT = pTb[pi % 4]
                    v3 = vb[g][:, :].rearrange("p (b c) -> p b c", c=192)
                    for ai, (bk, ocols, pcols) in enumerate(avs):
                        st = not started[bk]
                        started[bk] = True
                        sp_ = last[bk] == (pi, ai)
                        mm(pb[bk][orows, ocols], v3[:, kb, lcols], pT[:, pcols], [vb[g], pT], [pb[bk]], start=st, stop=sp_, skip=True)

                for pi in range(len(plan) + SKEW):
                    if pi < len(plan):
                        emit_scores(pi)
                    if pi - SKEW >= 0:
                        emit_av(pi - SKEW)
                for bk in range(4):
                    t1_ = t1 if bk % 2 == 0 else ph["t1b"]
                    bcs_ = bcs if bk % 2 == 0 else ph["bcsb"]
                    act(rden[drow:drow + 1, :], pb[bk][drow:drow + 1, :], AF.Ln, [pb[bk]], [rden])
                    act(rden[drow:drow + 1, :], rden[drow:drow + 1, :], AF.Exp, [rden], [rden], scale=-1.0)
                    bb = pb[4 + bk % 2]
                    mm(bb[nrows, :], onesf[drow:drow + 1, 0:64], rden[drow:drow + 1, :], [onesf, rden], [bb])
                    acopy(bcs_[nrows, :], bb[nrows, :], [bb], [bcs_])
                    tt("dve", t1_[nrows, :], pb[bk][nrows, :], bcs_[nrows, :], ALU.mult, [pb[bk], bcs_], [t1_])
                    tt("pool", oaT[hp][nrows, bk * 512:(bk + 1) * 512], t1_[nrows, :], gz[nrows, bk * 512:(bk + 1) * 512], ALU.mult, [t1_, gz], [oaT[hp]])

        P.phase = "R%d" % s
        specs = [("rA", 514, F32), ("rB", 512, F32), ("rr", 512, F32), ("kk0", 512, F32), ("vv", 512, F32),
                 ("rF", 512, F32), ("aa", 512, F32), ("rH", 514, F32), ("Pinv", 512, F32), ("Pex", 512, F32),
                 ("Pend", 512, F32), ("rL", 512, F32), ("beta", 512, F32), ("rN", 512, F32),
                 ("bon0", 512, F32), ("bon1", 512, F32), ("lowm", 512, F32), ("lowbf", 512, BF16), ("PCc0", 8, F32), ("PCc1", 8, F32),
                 ("sq", 512, BF16),
                 ("ALRT", 2048, BF16), ("BTKT", 2048, BF16), ("BH", 1024, BF16), ("KH", 1024, BF16), ("VB", 1024, BF16),
                 ("Yb", 512, F32), ("yc", 512, F32), ("zr", 512, F32), ("Ybf", 512, BF16), ("Ut", 128, BF16)]
        for g_ in range(4):
            specs += [("AbAr%d" % g_, 512, BF16), ("TRg%d" % g_, 1024, BF16), ("QWg%d" % g_, 512, BF16),
                      ("ATg%d" % g_, 512, BF16), ("TXa%d" % g_, 512, BF16), ("TXb%d" % g_, 512, BF16)]
        for g_ in range(2):
            specs += [("Akr%d" % g_, 512, BF16), ("XTa%d" % g_, 512, BF16), ("XTb%d" % g_, 512, BF16)]
        ph = new_phase(specs)
        R = ph
        for g_ in range(4):
            for ab in "ab":
                t_ = R["TX%s%d" % (ab, g_)]
                x_ = Buf(t_.t, "TX%sx%d" % (ab, g_))
                x_.readers = dict(t_.readers)
                R["TX%sx%d" % (ab, g_)] = x_
        mu2 = Buf(cstf[:, 256:512], "mu2")
        ml2 = Buf(cstf[:, 512:768], "ml2")
        mu2.last_write = cstf.last_write
        ml2.last_write = cstf.last_write
        for nm in ("ALRT", "BTKT", "BH", "KH", "VB"):
            P.op("pool", lambda e, b=R[nm]: e.memset(b[:, :], 0.0), [], [R[nm]])
        P.op("dve", lambda e: e.memset(prevc[:, :], 0.0), [], [prevc])
        for hp in range(4):
            P.op("pool", lambda e, b=Sf[hp]: e.memset(b[:, :], 0.0), [], [Sf[hp]])
            P.op("pool", lambda e, b=Sb[hp]: e.memset(b[:, :], 0.0), [], [Sb[hp]])
        hb = pb[0:6]
        rctr = [0]

        def rbank():
            rctr[0] += 1
            return hb[rctr[0] % 4]

        gpc = [0]

        def gp_bank():
            gpc[0] += 1
            return pb[6 + gpc[0] % 2]

        raw, dtmp = R["rA"], R["rB"]

        def shifted_proj_g(tile, pidx, tok0, dst, mixcol):
            slot = get_w(tile)
            bank = gp_bank()
            proj_fm(slot, tok0, 512, bank)
            yield
            vcopy("dve", raw[:, 0:1], prevc[:, pidx:pidx + 1], [prevc], [raw])
            acopy(raw[:, 1:513], bank[:, :], [bank], [raw])
            yield
            vcopy("dve", prevc[:, pidx:pidx + 1], raw[:, 512:513], [raw], [prevc])
            omix = PC_OMIX + (mixcol - PC_MIX)
            act(dtmp[:, :], raw[:, 1:513], AF.Identity, [raw, pcol], [dtmp], scale=pcol[:, omix:omix + 1], bias=0.0)
            yield
            stt(dst[:, :], raw[:, 0:512], pcol[:, mixcol:mixcol + 1], dtmp[:, :], ALU.mult, ALU.add, [dtmp, pcol, raw], [dst])
            yield

        def c3(buf, w=64):
            return buf[:, :].rearrange("p (c t) -> p c t", t=w)

        ALRT3 = R["ALRT"][:, :].rearrange("p (c w) -> p c w", w=256)
        BTKT3 = R["BTKT"][:, :].rearrange("p (c w) -> p c w", w=256)
        BH3 = R["BH"][:, :].rearrange("p (c w) -> p c w", w=128)
        KH3 = R["KH"][:, :].rearrange("p (c w) -> p c w", w=128)
        VB3 = R["VB"][:, :].rearrange("p (c w) -> p c w", w=128)

        def gP(hp, tok0, first, fl):
            rr, kk0, vv = R["rr"], R["kk0"], R["vv"]
            if first:
                yield from shifted_proj_g(52, 12, tok0, R["lowm"], PC_MIX + 12)
                act(R["lowbf"][0:64, :], R["lowm"][0:64, :], AF.Tanh, [R["lowm"]], [R["lowbf"]])
                vcopy("dve", R["lowbf"][64:128, :], R["lowm"][64:128, :], [R["lowm"]], [R["lowbf"]])
            fl["low"] = True
            yield
            yield from shifted_proj_g(44 + hp, 4 + hp, tok0, kk0, PC_MIX + 4 + hp)
            fl["kk0"] = True
            yield from shifted_proj_g(40 + hp, hp, tok0, rr, PC_MIX + hp)
            yield from shifted_proj_g(48 + hp, 8 + hp, tok0, vv, PC_MIX + 8 + hp)
            fl["P"] = True

        def gD(hp, par, fl):
            while not fl.get("low"):
                yield
            sg, aa, csg = R["rF"], R["aa"], R["rH"]
            bw = pb[4]
            mm(bw[:, :], lora[0:64, hp * 128:(hp + 1) * 128], R["lowbf"][0:64, :], [lora, R["lowbf"]], [bw])
            yield
            act(sg[:, :], bw[:, :], AF.Sigmoid, [bw, pcol], [sg], bias=pcol[:, PC_W0 + hp:PC_W0 + hp + 1], scale=1.0)
            yield
            ba = pb[4]
            mm(ba[:, :], lora[64:128, hp * 128:(hp + 1) * 128], R["lowbf"][64:128, :], [lora, R["lowbf"]], [ba])
            yield
            act(aa[:, :], ba[:, :], AF.Sigmoid, [ba, pcol], [aa], bias=pcol[:, PC_A0 + hp:PC_A0 + hp + 1], scale=1.0)
            fl["aa"] = True
            yield
            P.op("dve", lambda e: e.memset(csg[:, 0:1], 0.0), [], [csg])
            P.op("dve", lambda e: e.tensor_tensor_scan(out=csg[:, 1:513], data0=ones5[:, :], data1=sg[:, :], initial=0.0,
                                                       op0=ALU.mult, op1=ALU.add), [ones5, sg], [csg])
            yield
            while not fl.get("P"):
                yield
            ci, ce, dts = R["rA"], R["rB"], R["rF"]
            base = csg[:, 0:512].rearrange("p (c t) -> p c t", t=64)[:, :, 0:1].broadcast_to([128, 8, 64])
            tt("dve", ci[:, 0:512].rearrange("p (c t) -> p c t", t=64), csg[:, 1:513].rearrange("p (c t) -> p c t", t=64), base, ALU.subtract, [csg], [ci])
            yield
            tt("dve", c3(ce), csg[:, 0:512].rearrange("p (c t) -> p c t", t=64), base, ALU.subtract, [csg], [ce])
            yield
            ci3 = ci[:, 0:512].rearrange("p (c t) -> p c t", t=64)
            tt("dve", c3(dts), ci3[:, :, 63:64].broadcast_to([128, 8, 64]), ci3, ALU.subtract, [ci], [dts])
            yield
            Pm, Pinv, Pex, Pend = R["rH"], R["Pinv"], R["Pex"], R["Pend"]
            PCc = R["PCc%d" % par]
            act(Pinv[:, :], ci[:, 0:512], AF.Exp, [ci], [Pinv], scale=C0)
            yield
            act(Pex[:, :], ce[:, :], AF.Exp, [ce], [Pex], scale=-C0)
            yield
            act(Pend[:, :], dts[:, :], AF.Exp, [dts], [Pend], scale=-C0)
            act(PCc[:, 0:8], ci[:, 63:512:64], AF.Exp, [ci], [PCc], scale=-C0)
            yield
            act(Pm[:, 0:512], ci[:, 0:512], AF.Exp, [ci, csg], [Pm], scale=-C0)
            fl["D"] = True

        def gK(hp, fl):
            kk0 = R["kk0"]
            while not fl.get("kk0"):
                yield
            nrm = R["rL"]
            act(R["sq"][:, :], kk0[:, :], AF.Square, [kk0, pcol], [R["sq"]], scale=pcol[:, PC_KK + hp:PC_KK + hp + 1])
            yield
            bs = pb[5]
            mm(bs[:, :], blk1[:, :], R["sq"][:, :], [blk1, R["sq"]], [bs])
            yield
            ts("dve", nrm[:, :], bs[:, :], 1e-24, None, ALU.max, None, [bs], [nrm])
            yield
            act(nrm[:, :], nrm[:, :], AF.Ln, [nrm], [nrm])
            yield
            act(R["beta"][:, :], nrm[:, :], AF.Exp, [nrm], [R["beta"]], scale=-0.5)
            yield
            kkn = nrm
            stt(kkn[:, :], kk0[:, :], pcol[:, PC_KK + hp:PC_KK + hp + 1], R["beta"][:, :], ALU.mult, ALU.mult, [kk0, pcol, R["beta"]], [kkn])
            yield
            while not fl.get("aa"):
                yield
            aa = R["aa"]
            beta, kmod = R["beta"], R["rN"]
            tt("dve", beta[:, :], kkn[:, :], aa[:, :], ALU.mult, [kkn, aa], [beta])
            yield
            act(kmod[:, :], aa[:, :], AF.Identity, [aa, pcol], [kmod],
                scale=pcol[:, PC_KA + hp:PC_KA + hp + 1], bias=pcol[:, PC_OMKA + hp:PC_OMKA + hp + 1])
            yield
            tt("pool", kmod[:, :], kk0[:, :], kmod[:, :], ALU.mult, [kk0, kmod], [kmod])
            fl["K"] = True

        def r1b(hp, par, fl, prevfl):
            while not (fl.get("P") and fl.get("D") and fl.get("K")):
                yield
            rr, kk0, vv = R["rr"], R["kk0"], R["vv"]
            kkn, beta, kmod = R["rL"], R["beta"], R["rN"]
            Pm, Pinv, Pex, Pend = R["rH"], R["Pinv"], R["Pex"], R["Pend"]
            bon = R["bon%d" % par]
            for hh in range(2):
                rs_ = slice(64 * hh, 64 * hh + 64)
                co = 64 * hh

                def d3(buf, rs_=rs_):
                    return buf[rs_, 0:512].rearrange("p (c t) -> p c t", t=64)

                tt("dve", BTKT3[rs_, :, co:co + 64], d3(beta), d3(Pinv), ALU.mult, [beta, Pinv], [R["BTKT"]])
                tt("dve", BTKT3[rs_, :, 128 + co:128 + co + 64], d3(kmod), d3(Pinv), ALU.mult, [kmod, Pinv], [R["BTKT"]])
                yield
                tt("pool", BH3[rs_, :, co:co + 64], d3(beta), d3(Pend), ALU.mult, [beta, Pend], [R["BH"]])
                tt("pool", KH3[rs_, :, co:co + 64], d3(kmod), d3(Pend), ALU.mult, [kmod, Pend], [R["KH"]])
                acopy(VB3[rs_, :, co:co + 64], d3(vv), [vv], [R["VB"]])
                yield
            stt(R["sq"][:, :], rr[:, :], pcol[:, PC_RK + hp:PC_RK + hp + 1], kmod[:, :], ALU.mult, ALU.mult, [rr, pcol, kmod], [R["sq"]])
            yield
            bbn = pb[5]
            mm(bbn[:, :], blk1[:, :], R["sq"][:, :], [blk1, R["sq"]], [bbn])
            yield
            tt("dve", bon[:, :], bbn[:, :], vv[:, :], ALU.mult, [bbn, vv], [bon])
            yield
            while not prevfl.get("r3done"):
                yield
            for hh in range(2):
                rs_ = slice(64 * hh, 64 * hh + 64)
                co = 64 * hh

                def d3(buf, rs_=rs_):
                    return buf[rs_, 0:512].rearrange("p (c t) -> p c t", t=64)

                stt(ALRT3[rs_, :, co:co + 64], d3(kkn), -1.0, d3(Pex), ALU.mult, ALU.mult, [kkn, Pex], [R["ALRT"]])
                tt("dve", ALRT3[rs_, :, 128 + co:128 + co + 64], d3(rr), d3(Pm), ALU.mult, [rr, Pm], [R["ALRT"]])
                yield

        bctr = [0]

        def bank8():
            bctr[0] += 1
            return pb[bctr[0] % 8]

        def evac(k, out_ap, in_ap, reads, writes):
            if k % 2 == 0:
                acopy(out_ap, in_ap, reads, writes)
            else:
                vcopy("dve", out_ap, in_ap, reads, writes)

        def r2(hp, cs):
            AbAr = [R["AbAr%d" % g_] for g_ in range(4)]
            ATg = [R["ATg%d" % g_] for g_ in range(4)]
            TRg = [R["TRg%d" % g_] for g_ in range(4)]
            QWg = [R["QWg%d" % g_] for g_ in range(4)]
            TX = [[R["TXa%d" % g_] for g_ in range(4)], [R["TXb%d" % g_] for g_ in range(4)]]
            TXx = [[R["TXax%d" % g_] for g_ in range(4)], [R["TXbx%d" % g_] for g_ in range(4)]]
            XT = [[R["XTa%d" % g_] for g_ in range(2)], [R["XTb%d" % g_] for g_ in range(2)]]
            Akr = [R["Akr%d" % g_] for g_ in range(2)]

            def s2(bufs, c, lo, hi, w=256):
                return bufs[c // 2][:, (c % 2) * w + lo:(c % 2) * w + hi]

            def s4(bufs, c, lo=0, hi=128):
                return bufs[c // 4][:, (c % 4) * 128 + lo:(c % 4) * 128 + hi]

            mu2b = mu2[:, :].unsqueeze(1).broadcast_to([128, 2, 256])
            ml2b = ml2[:, :].unsqueeze(1).broadcast_to([128, 2, 256])
            mu1b = mu2[:, 128:256].unsqueeze(1).broadcast_to([128, 4, 128])
            for g_ in range(4):
                p1 = bank8()
                for c in (2 * g_, 2 * g_ + 1):
                    o = (c % 2) * 256
                    mm(p1[:, o:o + 256], BTKT3[:, c, 0:128], ALRT3[:, c, :], [R["BTKT"], R["ALRT"]], [p1])
                tt("dve", AbAr[g_][:, :].rearrange("p (c w) -> p c w", w=256), p1[:, :].rearrange("p (c w) -> p c w", w=256), mu2b, ALU.mult, [p1, mu2], [AbAr[g_]])
                p3 = bank8()
                for c in (2 * g_, 2 * g_ + 1):
                    o = (c % 2) * 256
                    mm(p3[:, o:o + 256], ALRT3[:, c, 0:128], BTKT3[:, c, :], [R["BTKT"], R["ALRT"]], [p3])
                tt("dve", ATg[g_][:, :].rearrange("p (c w) -> p c w", w=256), p3[:, :].rearrange("p (c w) -> p c w", w=256), ml2b, ALU.mult, [p3, ml2], [ATg[g_]])
            yield
            for h_ in range(2):
                p2 = bank8()
                for c in range(4 * h_, 4 * h_ + 4):
                    o = (c % 4) * 128
                    mm(p2[:, o:o + 128], BTKT3[:, c, 128:256], ALRT3[:, c, 128:256], [R["BTKT"], R["ALRT"]], [p2])
                tt("dve", Akr[h_][:, :].rearrange("p (c w) -> p c w", w=128), p2[:, :].rearrange("p (c w) -> p c w", w=128), mu1b, ALU.mult, [p2, mu2], [Akr[h_]])
            for g_ in range(4):
                tt("pool", TX[0][g_][:, :].rearrange("p (c w) -> p c w", w=256)[:, :, 0:128],
                   AbAr[g_][:, :].rearrange("p (c w) -> p c w", w=256)[:, :, 0:128],
                   identb[:, :].unsqueeze(1).broadcast_to([128, 2, 128]), ALU.add, [AbAr[g_], identb], [TX[0][g_]])
            for h_ in range(2):
                pA = bank8()
                pB = bank8()
                for c in range(4 * h_, 4 * h_ + 4):
                    o = (c % 4) * 128
                    mm(pA[:, o:o + 128], s2(ATg, c, 0, 128), s2(AbAr, c, 0, 128), [ATg[c // 2], AbAr[c // 2]], [pA])
                for c in range(4 * h_, 4 * h_ + 4):
                    o = (c % 4) * 128
                    mm(pB[:, o:o + 128], s2(AbAr, c, 0, 128), s2(ATg, c, 0, 128), [ATg[c // 2], AbAr[c // 2]], [pB])
                for gg in (2 * h_, 2 * h_ + 1):
                    o = (gg % 2) * 256
                    evac(h_, TX[0][gg][:, :].rearrange("p (c w) -> p c w", w=256)[:, :, 128:256],
                         pA[:, o:o + 256].rearrange("p (c w) -> p c w", w=128), [pA], [TXx[0][gg]])
                evac(h_ + 1, XT[0][h_][:, :], pB[:, :], [pB], [XT[0][h_]])
            yield
            cur = 0
            for lvl in range(2, 6):
                nxt = 1 - cur
                pBs = [bank8(), bank8()]
                for g_ in range(4):
                    pA = bank8()
                    for c in (2 * g_, 2 * g_ + 1):
                        o = (c % 2) * 256
                        mm(pA[:, o:o + 128], identb[:, :], s2(TX[cur], c, 0, 128), [identb, TX[cur][g_]], [pA], start=True, stop=False, skip=True)
                        mm(pA[:, o:o + 256], s4(XT[cur], c), s2(TX[cur], c, 0, 256), [XT[cur][c // 4], TX[cur][g_], TXx[cur][g_]], [pA], start=False, stop=True, skip=True)
                    for c in (2 * g_, 2 * g_ + 1):
                        o = (c % 4) * 128
                        mm(pBs[c // 4][:, o:o + 128], s2(TX[cur], c, 128, 256), s4(XT[cur], c), [XT[cur][c // 4], TXx[cur][g_]], [pBs[c // 4]])
                    evac(g_ + lvl, TX[nxt][g_][:, :], pA[:, :], [pA], [TX[nxt][g_], TXx[nxt][g_]])
                    if g_ % 2 == 1:
                        h_ = g_ // 2
                        evac(h_ + lvl + 1, XT[nxt][h_][:, :], pBs[h_][:, :], [pBs[h_]], [XT[nxt][h_]])
                cur = nxt
                yield
            nxt = 1 - cur
            for h_ in range(2):
                pA = bank8()
                for c in range(4 * h_, 4 * h_ + 4):
                    o = (c % 4) * 128
                    mm(pA[:, o:o + 128], identb[:, :], s2(TX[cur], c, 0, 128), [identb, TX[cur][c // 2]], [pA], start=True, stop=False, skip=True)
                    mm(pA[:, o:o + 128], s4(XT[cur], c), s2(TX[cur], c, 0, 128), [XT[cur][c // 4], TX[cur][c // 2]], [pA], start=False, stop=True, skip=True)
                for gg in (2 * h_, 2 * h_ + 1):
                    o = (gg % 2) * 256
                    evac(h_ + 1, TX[nxt][gg][:, :].rearrange("p (c w) -> p c w", w=256)[:, :, 0:128],
                         pA[:, o:o + 256].rearrange("p (c w) -> p c w", w=128), [pA], [TX[nxt][gg]])
            for g_ in range(4):
                pt = bank8()
                ptb = pt[:, :].bitcast(BF16)
                for c in (2 * g_, 2 * g_ + 1):
                    for i_, src in enumerate((ALRT3[:, c, 0:128], BH3[:, c, :], KH3[:, c, :], VB3[:, c, :])):
                        o = (c % 2) * 512 + i_ * 128
                        P.op("pe", lambda e, o=o, src=src, ptb=ptb: e.transpose(out=ptb[:, o:o + 128], in_=src, identity=identb[:, :]),
                             [R["ALRT"], R["BH"], R["KH"], R["VB"], identb], [pt])
                evac(g_, TRg[g_][:, :], ptb[:, :], [pt], [TRg[g_]])
            yield
            for g_ in range(4):
                pq = bank8()
                for c in (2 * g_, 2 * g_ + 1):
                    o = (c % 2) * 256
                    Tf_ = s2(TX[nxt], c, 0, 128)
                    mm(pq[:, o:o + 128], s2(ATg, c, 128, 256), Tf_, [ATg[g_], TX[nxt][g_]], [pq])
                    mm(pq[:, o + 128:o + 256], s2(TRg, c, 0, 128, w=512), Tf_, [TRg[g_], TX[nxt][g_]], [pq])
                evac(g_ + 1, QWg[g_][:, :], pq[:, :], [pq], [QWg[g_]])
            yield

        def r3(hp, cs, par, myfl):
            Ut = R["Ut"]
            PCc = R["PCc%d" % par]
            for c in cs:
                AbArB, AkrB, TRB, QWB = R["AbAr%d" % (c // 2)], R["Akr%d" % (c // 4)], R["TRg%d" % (c // 2)], R["QWg%d" % (c // 2)]
                AbAr = Buf(AbArB[:, (c % 2) * 256:(c % 2) * 256 + 256], "v")
                Akr = Buf(AkrB[:, (c % 4) * 128:(c % 4) * 128 + 128], "v")
                TR = Buf(TRB[:, (c % 2) * 512:(c % 2) * 512 + 512], "v")
                QW = Buf(QWB[:, (c % 2) * 256:(c % 2) * 256 + 256], "v")
                pu = rbank()
                mm(pu[:, 0:128], QW[:, 0:128], TR[:, 384:512], [QWB, TRB], [pu], start=True, stop=False)
                mm(pu[:, 0:128], QW[:, 128:256], Sb[hp][:, :], [QWB, Sb[hp]], [pu], start=False, stop=True)
                acopy(Ut[:, :], pu[:, 0:128], [pu], [Ut])
                yield
                py = rbank()
                mm(py[:, 0:128], Sb[hp][:, :], ALRT3[:, c, 128:256], [Sb[hp], R["ALRT"]], [py], start=True, stop=False)
                mm(py[:, 0:128], TR[:, 384:512], Akr[:, :], [TRB, AkrB], [py], start=False, stop=False)
                mm(py[:, 0:128], Ut[:, :], AbAr[:, 128:256], [Ut, AbArB], [py], start=False, stop=True)
                pS = rbank()
                mm(pS[:, 0:128], TR[:, 256:384], TR[:, 384:512], [TRB], [pS], start=True, stop=False)
                mm(pS[:, 0:128], TR[:, 128:256], Ut[:, :], [TRB, Ut], [pS], start=False, stop=True)
                if c == cs[-1]:
                    myfl["r3done"] = True
                yield
                stt(Sb[hp][:, :], Sf[hp][:, :], PCc[:, c:c + 1], pS[:, 0:128], ALU.mult, ALU.add, [Sf[hp], PCc, pS], [Sb[hp]])
                stt(Sf[hp][:, :], Sf[hp][:, :], PCc[:, c:c + 1], pS[:, 0:128], ALU.mult, ALU.add, [Sf[hp], PCc, pS], [Sf[hp]])
                acopy(R["Yb"][0:64, c * 64:(c + 1) * 64], py[0:64, 0:64], [py], [R["Yb"]])
                acopy(R["Yb"][64:128, c * 64:(c + 1) * 64], py[64:128, 64:128], [py], [R["Yb"]])
                yield

        def r4(hp, tok0, par):
            Yb, yc = R["Yb"], R["yc"]
            bon = R["bon%d" % par]
            acopy(R["Ybf"][:, :], Yb[:, :], [Yb], [R["Ybf"]])
            slot = get_w(53 + hp)
            gbank = rbank()
            proj_fm(slot, tok0, 512, gbank)
            pm = rbank()
            mm(pm[:, :], blkavg[:, :], R["Ybf"][:, :], [blkavg, R["Ybf"]], [pm])
            yield
            act(R["zr"][:, :], gbank[:, :], AF.Silu, [gbank], [R["zr"]])
            tt("dve", yc[:, :], Yb[:, :], pm[:, :], ALU.subtract, [Yb, pm], [yc])
            act(R["Ybf"][:, :], yc[:, :], AF.Square, [yc], [R["Ybf"]])
            pv = rbank()
            mm(pv[:, :], blkavg[:, :], R["Ybf"][:, :], [blkavg, R["Ybf"]], [pv])
            yield
            rstd = Yb
            act(rstd[:, :], pv[:, :], AF.Ln, [pv], [rstd], bias=GN_EPS, scale=1.0)
            act(rstd[:, :], rstd[:, :], AF.Exp, [rstd], [rstd], scale=-0.5)
            tt("dve", yc[:, :], yc[:, :], rstd[:, :], ALU.mult, [yc, rstd], [yc])
            yield
            act(yc[:, :], yc[:, :], AF.Identity, [yc, pcol], [yc],
                scale=pcol[:, PC_LNW + hp:PC_LNW + hp + 1], bias=pcol[:, PC_LNB + hp:PC_LNB + hp + 1])
            tt("dve", yc[:, :], yc[:, :], bon[:, :], ALU.add, [yc, bon], [yc])
            tt("dve", orT[hp][:, tok0:tok0 + 512], yc[:, :], R["zr"][:, :], ALU.mult, [yc, R["zr"]], [orT[hp]])
            yield

        def chain(*gens):
            for g_ in gens:
                yield from g_

        def lab(name, g_):
            while True:
                P.phase = name
                try:
                    next(g_)
                except StopIteration:
                    return
                yield

        SEQM = ""
        R1W = 4

        def interleave(gens, tag=""):
            gens = list(gens)
            if tag in SEQM or "1" in SEQM:
                for g_ in gens:
                    for _ in g_:
                        pass
                return
            while gens:
                for g_ in list(gens):
                    try:
                        next(g_)
                    except StopIteration:
                        gens.remove(g_)

        units = [(blk, hp) for blk in range(4) for hp in range(4)]

        def interleave_gen(gens):
            gens = list(gens)
            while gens:
                for g_ in list(gens):
                    try:
                        next(g_)
                        yield
                    except StopIteration:
                        gens.remove(g_)

        def take(g_, n):
            done = False
            while not done:
                for _ in range(n):
                    try:
                        next(g_)
                    except StopIteration:
                        done = True
                        break
                if not done:
                    yield

        ufl = [dict() for _ in units]

        def mk_r1(u):
            blk, hp = units[u]
            fl = {}
            par = u % 2
            prevfl = ufl[u - 1] if u > 0 else {"r3done": True}
            return [lab("r1a", gP(hp, blk * 512, hp == 0, fl)), lab("r1a", gD(hp, par, fl)), lab("r1a", gK(hp, fl)),
                    lab("r1b", r1b(hp, par, fl, prevfl))]

        interleave(mk_r1(0))
        ALLC = [0, 1, 2, 3, 4, 5, 6, 7]
        for u, (blk, hp) in enumerate(units):
            par = u % 2
            tok0 = blk * 512
            interleave([lab("r2", r2(hp, ALLC))], "a")
            tail = chain(lab("r3", r3(hp, ALLC, par, ufl[u])), lab("r4", r4(hp, tok0, par)))
            if u + 1 < len(units):
                interleave([tail, take(interleave_gen(mk_r1(u + 1)), R1W)], "c")
            else:
                interleave([tail])

        P.phase = "F%d" % s
        ph = new_phase([("sga", 512, F32), ("sgb", 512, F32), ("f1", 512, F32), ("f2", 512, F32), ("mT", 8 * 512, BF16),
                        ("ysb0", 1024, F32), ("ysb1", 1024, F32), ("xr0", 1024, F32), ("xr1", 1024, F32), ("fj", 512, BF16), ("pgain", 1024, F32)])
        pgain = ph["pgain"]
        P.dma("pool", pgain[:, :], pg_d[0:1, :].broadcast_to([128, D]), reads=[pg_d], writes=[pgain], sembuf=pgain)
        sga, sgb, f1, f2, mT, fj = ph["sga"], ph["sgb"], ph["f1"], ph["f2"], ph["mT"], ph["fj"]
        ysb = [ph["ysb0"], ph["ysb1"]]
        xr = [ph["xr0"], ph["xr1"]]
        mT3 = mT[:, :].rearrange("p (k t) -> p k t", t=512)
        fctr = [0]

        def fbank():
            fctr[0] += 1
            return pb[fctr[0] % 6]

        for blk in range(4):
            tok0 = blk * 512
            for dt in range(8):
                slot = get_w(57 + dt)
                bank = pbank()
                proj_fm(slot, tok0, 512, bank)
                act(sga[:, :], bank[:, :], AF.Sigmoid, [bank], [sga])
                slot = get_w(65 + dt)
                bank = pbank()
                proj_fm(slot, tok0, 512, bank)
                act(sgb[:, :], bank[:, :], AF.Sigmoid, [bank], [sgb])
                ya = fbank()
                for fc in range(4):
                    mm(ya[:, :], wua3[:, fc, dt * 128:(dt + 1) * 128], oaT[fc][:, tok0:tok0 + 512], [wua, oaT[fc]], [ya], start=(fc == 0), stop=(fc == 3))
                yb_ = fbank()
                for fc in range(4):
                    mm(yb_[:, :], wur3[:, fc, dt * 128:(dt + 1) * 128], orT[fc][:, tok0:tok0 + 512], [wur, orT[fc]], [yb_], start=(fc == 0), stop=(fc == 3))
                tt("dve", f1[:, :], ya[:, :], sga[:, :], ALU.mult, [ya, sga], [f1])
                tt("dve", f2[:, :], yb_[:, :], sgb[:, :], ALU.mult, [yb_, sgb], [f2])
                tt("pool", mT3[:, dt, :], f1[:, :], f2[:, :], ALU.add, [f1, f2], [mT])
            for t4 in range(4):
                ti = blk * 4 + t4
                yt = ysb[ti % 2]
                xt = xr[ti % 2]
                P.dma("pool", xt[:, :], x_d[s, ti * 128:(ti + 1) * 128, :], reads=[x_d], writes=[xt], sembuf=xt)
                banks = [fbank(), fbank()]
                for hf in range(2):
                    for dc in range(8):
                        mm(banks[hf][:, :], mT3[:, dc, t4 * 128:(t4 + 1) * 128], wout3[:, dc, hf * 512:(hf + 1) * 512], [mT, wout], [banks[hf]],
                           start=(dc == 0), stop=(dc == 7))
                    act(fj[:, :], banks[hf][:, :], AF.Square, [banks[hf]], [fj, small], accum_out=small[:, 4 + hf:5 + hf])
                tt("dve", small[:, 6:7], small[:, 4:5], small[:, 5:6], ALU.add, [small], [small])
                act(small[:, 7:8], small[:, 6:7], AF.Sqrt, [small], [small], scale=1.0 / D, bias=RMS_EPS)
                recip(small[:, 8:9], small[:, 7:8], [small], [small])
                for hf in range(2):
                    stt(yt[:, hf * 512:(hf + 1) * 512], banks[hf][:, :], small[:, 8:9], pgain[:, hf * 512:(hf + 1) * 512], ALU.mult, ALU.mult,
                        [banks[hf], small, pgain], [yt])
                tt("pool", yt[:, :], yt[:, :], xt[:, :], ALU.add, [yt, xt], [yt])
                tok = P.dma("pool", out_d[s, ti * 128:(ti + 1) * 128, :], yt[:, :], reads=[yt], writes=[out_d], sembuf=yt)
                out_toks.append(tok)

    P.wait_all("pool", out_toks)
    P.emit()
    P.close()
    nc._labels = P.labels
    return nc


_CACHE = {}


def _host_inputs(inp):
    f = lambda a: np.ascontiguousarray(np.asarray(a, dtype=np.float32))
    pcol = np.zeros((128, 80), np.float32)
    pcol[:, 0:8] = f(inp["pre_norm_gain"])[0].reshape(8, 128).T
    pcol[:, 8:21] = f(inp["rwkv_shift_mix"])[0].reshape(13, 128).T
    for off, key in ((21, "rwkv_w0"), (25, "rwkv_a0"), (29, "rwkv_k_k"), (33, "rwkv_k_a"), (37, "rwkv_r_k"),
                     (41, "rwkv_ln_w"), (45, "rwkv_ln_b")):
        pcol[:, off:off + 4] = f(inp[key])[0].reshape(4, 128).T
    c = _const_tables()
    cst = np.concatenate([c["ident"], c["blk1"], c["mu2"], c["ml2"]], axis=1).astype(np.float32)
    shared = {
        "w_in": f(inp["w_in"])[0],
        "biasm": _bias_tables(f(inp["rel_bias"])),
        "pcol": pcol,
        "pgain": f(inp["post_norm_gain"])[0].reshape(1, D),
        "lora": np.concatenate([f(inp["rwkv_w_up"])[0], f(inp["rwkv_a_up"])[0]], axis=0),
        "wua": f(inp["w_up_attn"])[0],
        "wur": f(inp["w_up_rwkv"])[0],
        "wout": f(inp["w_out"])[0],
        "cst": cst,
    }
    return shared


def kernel(**inputs):
    x = np.ascontiguousarray(np.asarray(inputs["x"], dtype=np.float32))
    shared = _host_inputs(inputs)
    if "nc" not in _CACHE:
        _CACHE["nc"] = build(NSEQ)
    nc = _CACHE["nc"]
    in_maps = []
    for c in range(NCORES):
        m = dict(shared)
        m["x"] = x[c * NSEQ:(c + 1) * NSEQ]
        in_maps.append(m)
    res = run_bass_kernel_spmd(nc, in_maps, core_ids=list(range(NCORES)))
    return np.concatenate([r["out"] for r in res.results], axis=0)
```
